# Optimizing a Trainium2 kernel written in Bass

```python
import jax, jax.numpy as jnp
from jax import lax
import numpy as np

D_MODEL = 1024
BATCH = 4
SEQ = 4096
DEPTH = 2

N_CONV_LAYERS = DEPTH // 2
N_ATTN_LAYERS = DEPTH - N_CONV_LAYERS
CONV_WIDTH = 3
HEAD_DIM = 64
N_HEADS = D_MODEL // HEAD_DIM
BRANCHES = ((128, 1), (512, 4), (2048, 16))
N_BRANCHES = len(BRANCHES)
Q_WIDTH = N_BRANCHES * N_HEADS * HEAD_DIM
D_FF = -(-8 * D_MODEL // (3 * 256)) * 256
ROPE_THETA = 10000.0
RMS_EPS = 1e-6
NEG_INF = -1e30

kernel_name = "yoco_shortconv_dilated_attention_trunk"


def rms_norm(x, g):
    xf = x.astype(jnp.float32)
    y = xf * lax.rsqrt(jnp.mean(xf * xf, axis=-1, keepdims=True) + RMS_EPS)
    return (y * g.astype(jnp.float32)).astype(x.dtype)


def rope(t, positions):
    half = HEAD_DIM // 2
    inv_freq = ROPE_THETA ** (-jnp.arange(half, dtype=jnp.float32) / half)
    ang = positions.astype(jnp.float32)[..., None] * inv_freq
    cos = jnp.cos(ang)[:, :, None, :]
    sin = jnp.sin(ang)[:, :, None, :]
    t1 = t[..., :half].astype(jnp.float32)
    t2 = t[..., half:].astype(jnp.float32)
    out = jnp.concatenate([t1 * cos - t2 * sin, t2 * cos + t1 * sin], axis=-1)
    return out.astype(t.dtype)


def short_conv_mixer(x, w_in, conv_w, w_out):
    b_gate, c_gate, h = jnp.split(x @ w_in, 3, axis=-1)
    u = c_gate * h
    rhs = conv_w[:, None, :].astype(u.dtype)
    conv = lax.conv_general_dilated(
        u, rhs, window_strides=(1,), padding=[(CONV_WIDTH - 1, 0)],
        dimension_numbers=("NWC", "WIO", "NWC"), feature_group_count=u.shape[-1])
    return (b_gate * conv) @ w_out


def swiglu(x, w_gate_up, w_down):
    g, u = jnp.split(x @ w_gate_up, 2, axis=-1)
    return (jax.nn.silu(g) * u) @ w_down


def dilated_branch(q, k, v, window, dilation):
    band = window // dilation
    B, S, H, Dh = q.shape
    chunk = dilation * band
    Sp = -(-S // chunk) * chunk
    nb = Sp // chunk
    pad = ((0, 0), (0, Sp - S), (0, 0), (0, 0))

    def to_blocks(t):
        t = jnp.pad(t, pad).reshape(B, nb, band, dilation, H, Dh)
        return t.transpose(0, 3, 4, 1, 2, 5)

    def with_prev(t):
        prev = jnp.pad(t, ((0, 0), (0, 0), (0, 0), (1, 0), (0, 0), (0, 0)))[:, :, :, :-1]
        return jnp.concatenate([prev, t], axis=4)

    qb = to_blocks(q * (HEAD_DIM ** -0.5))
    kk = with_prev(to_blocks(k))
    vv = with_prev(to_blocks(v))
    s = jnp.einsum("brhnqd,brhnkd->brhnqk", qb, kk).astype(jnp.float32)
    qi = jnp.arange(band)[:, None]
    kj = jnp.arange(2 * band)[None, :]
    dist = qi + band - kj
    in_band = (dist >= 0) & (dist <= band)
    has_prev = (kj >= band)[None] | (jnp.arange(nb)[:, None, None] > 0)
    mask = in_band[None] & has_prev
    s = jnp.where(mask, s, NEG_INF)
    m = jnp.max(s, axis=-1)
    p = jnp.exp(s - m[..., None])
    l = jnp.sum(p, axis=-1)
    o = jnp.einsum("brhnqk,brhnkd->brhnqd", p, vv.astype(jnp.float32)) / l[..., None]
    lse = m + jnp.log(l)
    o = o.transpose(0, 3, 4, 1, 2, 5).reshape(B, Sp, H, Dh)[:, :S]
    lse = lse.transpose(0, 3, 4, 1, 2).reshape(B, Sp, H)[:, :S]
    return o, lse


def dilated_attention_mixer(x, positions, k_sh, v_sh, w_q, w_o):
    B, S, _ = x.shape
    q = (x @ w_q).reshape(B, S, N_BRANCHES * N_HEADS, HEAD_DIM)
    q = rope(q, positions).reshape(B, S, N_BRANCHES, N_HEADS, HEAD_DIM)
    outs, lses = [], []
    for g, (window, dilation) in enumerate(BRANCHES):
        o, lse = dilated_branch(q[:, :, g], k_sh[:, :, g], v_sh[:, :, g], window, dilation)
        outs.append(o)
        lses.append(lse)
    wts = jax.nn.softmax(jnp.stack(lses, axis=0), axis=0)
    o = jnp.einsum("gbsh,gbshd->bshd", wts, jnp.stack(outs, axis=0))
    return o.astype(x.dtype).reshape(B, S, N_HEADS * HEAD_DIM) @ w_o


def shared_kv(h, positions, kv_norm, w_kv):
    B, S, _ = h.shape
    kv = (rms_norm(h, kv_norm) @ w_kv).reshape(B, S, 2, N_BRANCHES * N_HEADS, HEAD_DIM)
    k = rope(kv[:, :, 0], positions).reshape(B, S, N_BRANCHES, N_HEADS, HEAD_DIM)
    v = kv[:, :, 1].reshape(B, S, N_BRANCHES, N_HEADS, HEAD_DIM)
    return k, v


def setup_inputs(seed: int = 0) -> dict:
    key = jax.random.key(seed)
    ks = jax.random.split(key, 20)
    f32 = jnp.float32

    def w(k, shape, fan_in):
        return jax.random.normal(k, shape, f32) * (fan_in ** -0.5)

    def gain(k, shape):
        return 1.0 + 0.05 * jax.random.normal(k, shape, f32)

    x = jax.random.normal(ks[0], (BATCH, SEQ, D_MODEL), f32)
    offset = jax.random.randint(ks[1], (BATCH, 1), 0, 4096, dtype=jnp.int32)
    positions = offset + jnp.arange(SEQ, dtype=jnp.int32)[None, :]
    nA, nB = N_CONV_LAYERS, N_ATTN_LAYERS
    return {
        "x": x,
        "positions": positions,
        "mix_norm_pre": gain(ks[2], (DEPTH, D_MODEL)),
        "mix_norm_post": gain(ks[3], (DEPTH, D_MODEL)),
        "ffn_norm_pre": gain(ks[4], (DEPTH, D_MODEL)),
        "ffn_norm_post": gain(ks[5], (DEPTH, D_MODEL)),
        "ffn_w_gate_up": w(ks[6], (DEPTH, D_MODEL, 2 * D_FF), D_MODEL),
        "ffn_w_down": w(ks[7], (DEPTH, D_FF, D_MODEL), D_FF),
        "conv_w_in": w(ks[8], (nA, D_MODEL, 3 * D_MODEL), D_MODEL),
        "conv_w": w(ks[9], (nA, CONV_WIDTH, D_MODEL), CONV_WIDTH),
        "conv_w_out": w(ks[10], (nA, D_MODEL, D_MODEL), D_MODEL),
        "kv_norm": gain(ks[11], (D_MODEL,)),
        "w_kv": w(ks[12], (D_MODEL, 2 * Q_WIDTH), D_MODEL),
        "w_q": w(ks[13], (nB, D_MODEL, Q_WIDTH), D_MODEL),
        "w_o": w(ks[14], (nB, N_HEADS * HEAD_DIM, D_MODEL), N_HEADS * HEAD_DIM),
    }


def reference(x, positions, mix_norm_pre, mix_norm_post, ffn_norm_pre, ffn_norm_post,
              ffn_w_gate_up, ffn_w_down, conv_w_in, conv_w, conv_w_out,
              kv_norm, w_kv, w_q, w_o):
    h = x
    for layer in range(DEPTH):
        if layer == N_CONV_LAYERS:
            k_sh, v_sh = shared_kv(h, positions, kv_norm, w_kv)
        hn = rms_norm(h, mix_norm_pre[layer])
        if layer < N_CONV_LAYERS:
            y = short_conv_mixer(hn, conv_w_in[layer], conv_w[layer], conv_w_out[layer])
        else:
            j = layer - N_CONV_LAYERS
            y = dilated_attention_mixer(hn, positions, k_sh, v_sh, w_q[j], w_o[j])
        h = h + rms_norm(y, mix_norm_post[layer])
        f = swiglu(rms_norm(h, ffn_norm_pre[layer]), ffn_w_gate_up[layer], ffn_w_down[layer])
        h = h + rms_norm(f, ffn_norm_post[layer])
    return h
```

```python
import numpy as np
import ml_dtypes
import concourse.bass as bass
import concourse.mybir as mybir
from concourse.bass_utils import run_bass_kernel_spmd

F32 = mybir.dt.float32
BF16 = mybir.dt.bfloat16
I32 = mybir.dt.int32
ALU = mybir.AluOpType
AF = mybir.ActivationFunctionType

D = 1024
KC = 8
TOWN = 2048
G = 1024
TT = 512
DFF = 2816
FC = 22
BRANCH_D = (1, 4, 16)
EPS = 1e-6
WSLOT = 6144
NWSLOT = 3
NEG = -30000.0
DEBUG = False
SAME_ENG_STRICT = True
DMA_TOTAL_WAIT = True
STOP_AFTER = 0
RSTD_LNEXP = True
DBG_BRANCHES = (0, 1, 2)

C_MIXPRE0, C_MIXPOST0, C_FFNPRE0, C_FFNPOST0 = 0, 8, 16, 24
C_MIXPRE1, C_MIXPOST1, C_FFNPRE1, C_FFNPOST1 = 32, 40, 48, 56
C_KVN, C_CONV, C_INVF, C_NSGN, C_EPS, C_NEG1 = 64, 72, 96, 97, 98, 99
NCONST = 104


class PB:
    def __init__(self, nc):
        self.nc = nc
        self.E = dict(pe=nc.tensor, act=nc.scalar, dve=nc.vector, pool=nc.gpsimd, sp=nc.sync)
        self.sem = {k: nc.alloc_semaphore("s_" + k) for k in self.E}
        self.cnt = {k: 0 for k in self.E}
        self.waited = {k: {} for k in self.E}
        self.st = {}
        self.dsem = {}
        self.bank_rr = 0

    def _need(self, eng, evs):
        for ev in evs:
            name, sem, val = ev[0], ev[1], ev[2]
            if ev[3] == "dma" and DMA_TOTAL_WAIT:
                val = self.dsem[name[2:]][1]
            if self.waited[eng].get(name, 0) < val:
                self.E[eng].wait_ge(sem, val)
                self.waited[eng][name] = val

    def _hazards(self, eng, reads, writes, is_dma):
        evs = []
        for k in reads:
            s = self.st.get(k)
            if s and s[0] is not None:
                evs.append(s[0])
        strict = SAME_ENG_STRICT and eng != "pe"
        for k in writes:
            s = self.st.get(k)
            if s:
                if s[0] is not None and (is_dma or strict or s[0][3] != eng):
                    evs.append(s[0])
                for ev in s[1].values():
                    if is_dma or strict or ev[3] != eng:
                        evs.append(ev)
        return evs

    def _commit(self, ev, reads, writes):
        for k in reads:
            s = self.st.setdefault(k, [None, {}])
            s[1][ev[0]] = ev
        for k in writes:
            self.st[k] = [ev, {}]

    def op(self, eng, reads, writes, fn):
        self._need(eng, self._hazards(eng, reads, writes, False))
        ins = fn(self.E[eng])
        self.cnt[eng] += 1
        ins.then_inc(self.sem[eng], 1)
        ev = (eng, self.sem[eng], self.cnt[eng], eng)
        self._commit(ev, reads, writes)

    def dma(self, q, out, in_, reads, writes, slot):
        self._need(q, self._hazards(q, reads, writes, True))
        if slot not in self.dsem:
            self.dsem[slot] = [self.nc.alloc_semaphore("d_" + slot), 0]
        d = self.dsem[slot]
        self.E[q].dma_start(out=out, in_=in_).then_inc(d[0], 16)
        d[1] += 16
        ev = ("d_" + slot, d[0], d[1], "dma")
        self._commit(ev, reads, writes)

    def barrier(self):
        evs = [(k, self.sem[k], self.cnt[k], k) for k in self.E if self.cnt[k] > 0]
        evs += [("d_" + s, d[0], d[1], "dma") for s, d in self.dsem.items() if d[1] > 0]
        for e in self.E:
            self._need(e, evs)
        self.st = {}


class Ctx:
    pass


_UNIQ = [0]


def sbt(nc, name, shape, dtype):
    _UNIQ[0] += 1
    return nc.sbuf_tensor("%s_%d" % (name, _UNIQ[0]), shape, dtype)


def _consts_setup(pb, cx, consts_d):
    nc = pb.nc
    cx.consts = nc.alloc_sbuf_tensor("consts_sb", [128, NCONST], F32)
    cx.ones = nc.alloc_sbuf_tensor("ones_bf", [128, 128], BF16)
    cx.wbuf = nc.alloc_sbuf_tensor("wbuf", [128, NWSLOT, WSLOT], BF16)
    cx.ps = nc.alloc_psum_tensor("ps_all", [128, 8, 512], F32)
    cx.wrr = 0
    cx.kst_rr = 0
    cx.pref = {}
    cx.vrr = 0
    pb.dma("sp", cx.consts[:], consts_d, [], ["consts"], "consts")
    pb.op("dve", [], ["ones"], lambda e: e.memset(cx.ones[:], 1.0))


def gcol(cx, base, c):
    return cx.consts[:, base + c:base + c + 1]


def load_w(pb, cx, w_ap, kch, ranges):
    slot = cx.wrr % NWSLOT
    cx.wrr += 1
    total = sum(n for _, n in ranges)
    assert kch * total <= WSLOT
    view = cx.wbuf[:, slot, 0:kch * total].rearrange("p (k n) -> p k n", k=kch)
    wv = w_ap.rearrange("(k p) n -> p k n", p=128)
    off = 0
    for i, (c0, n) in enumerate(ranges):
        pb.dma("pool", view[:, :, off:off + n], wv[:, :, c0:c0 + n], [], [("w", slot, i)], "w%d" % slot)
        off += n
    keys = [("w", slot, i) for i in range(len(ranges))]
    return view, keys


def prefetch_w(pb, cx, w_ap, kch, ranges):
    cx.pref[(w_ap.tensor.name, kch, tuple(ranges))] = load_w(pb, cx, w_ap, kch, ranges)


def mm_stage(pb, cx, w_ap, kch, blocks, groups, tiles, xin, consume, banksets):
    rr = 0

    def get_w(bi):
        pk = (w_ap.tensor.name, kch, tuple(blocks[bi]))
        if bi == 0 and pk in cx.pref:
            return cx.pref.pop(pk)
        return load_w(pb, cx, w_ap, kch, blocks[bi])
    ahead = []
    nreq = 0
    for bi, ranges in enumerate(blocks):
        while nreq < len(blocks) and nreq <= bi + NWSLOT - 2:
            ahead.append(get_w(nreq))
            nreq += 1
        view, wkeys = ahead.pop(0)
        for (tid, W) in tiles:
            for gi, grp in enumerate(groups):
                banks = banksets[rr % len(banksets)]
                rr += 1
                assert len(banks) >= len(grp)
                rkeys = list(wkeys) + [xin(k, tid)[1] for k in range(kch)]
                wk = [("ps", banks[i]) for i in range(len(grp))]

                def fn(e, grp=grp, banks=banks, tid=tid, W=W, view=view):
                    ins = None
                    for i, ch in enumerate(grp):
                        for k in range(kch):
                            ins = e.matmul(cx.ps[:, banks[i], 0:W], lhsT=view[:, k, ch * 128:(ch + 1) * 128],
                                           rhs=xin(k, tid)[0], start=(k == 0), stop=(k == kch - 1))
                    return ins
                pb.op("pe", rkeys, wk, fn)
                consume(bi, gi, tid, W, [(cx.ps[:, banks[i], 0:W], ("ps", banks[i])) for i in range(len(grp))])


def rstd_from_sq(pb, cx, sq, sqkeys, nch, W, rstd_ap, rstd_key, bank):
    def fn(e):
        ins = None
        for c in range(nch):
            ins = e.matmul(cx.ps[:, bank, 0:W], lhsT=cx.ones[:], rhs=sq(c), start=(c == 0), stop=(c == nch - 1))
        return ins
    pb.op("pe", list(sqkeys) + ["ones"], [("ps", bank)], fn)
    if RSTD_LNEXP:
        pb.op("act", [("ps", bank), "consts"], [rstd_key],
              lambda e: e.activation(out=rstd_ap, in_=cx.ps[:, bank, 0:W], func=AF.Ln,
                                     bias=cx.consts[:, C_EPS:C_EPS + 1], scale=1.0 / D))
        pb.op("act", [rstd_key], [rstd_key], lambda e: e.activation(out=rstd_ap, in_=rstd_ap, func=AF.Exp, scale=-0.5))
    else:
        pb.op("act", [("ps", bank), "consts"], [rstd_key],
              lambda e: e.activation(out=rstd_ap, in_=cx.ps[:, bank, 0:W], func=AF.Sqrt,
                                     bias=cx.consts[:, C_EPS:C_EPS + 1], scale=1.0 / D))
        pb.op("dve", [rstd_key], [rstd_key], lambda e: e.reciprocal(out=rstd_ap, in_=rstd_ap))


def prenorm_tile(pb, cx, src, W, sqbuf, rstd_ap, rstd_key, bank, outs):
    for c in range(KC):
        a, k = src(c)
        pb.op("act", [k], [("sq", c)], lambda e, a=a, c=c: e.activation(out=sqbuf[:, c, 0:W], in_=a, func=AF.Square))
    rstd_from_sq(pb, cx, lambda c: sqbuf[:, c, 0:W], [("sq", c) for c in range(KC)], KC, W, rstd_ap, rstd_key, bank)
    for (gb, dst, eng) in outs:
        for c in range(KC):
            a, k = src(c)
            da, dk = dst(c)
            pb.op(eng, [k, rstd_key, "consts"], [dk],
                  lambda e, a=a, da=da, c=c, gb=gb: e.scalar_tensor_tensor(
                      out=da, in0=a, scalar=gcol(cx, gb, c), in1=rstd_ap, op0=ALU.mult, op1=ALU.mult))


def evac_y(pb, cx, bank_ap, bank_key, W, y_ap, y_key, sq_ap, sq_key, gain_ap):
    pb.op("act", [bank_key, "consts"], [y_key], lambda e: e.activation(out=y_ap, in_=bank_ap, func=AF.Copy, scale=gain_ap))
    pb.op("act", [bank_key], [sq_key], lambda e: e.activation(out=sq_ap, in_=bank_ap, func=AF.Square))


def ffn_group(pb, cx, h, g0, layer, w_gu, w_dn, gb_pre, gb_post, nxt=None):
    nc = pb.nc
    tiles = [(g0 // TT + i, TT) for i in range(G // TT)]
    with sbt(nc, "a_buf", [128, FC, G], BF16) as a_buf:
        with sbt(nc, "xn", [128, KC, G], BF16) as xn, sbt(nc, "sq", [128, KC, TT], BF16) as sqb, \
                sbt(nc, "rstd", [128, 2, TT], F32) as rstd, sbt(nc, "sg", [128, 2, TT], F32) as sg:
            for i, (tid, W) in enumerate(tiles):
                t0 = tid * TT
                prenorm_tile(pb, cx, lambda c: (h[:, c, t0:t0 + W], ("h", c, tid)), W, sqb, rstd[:, i % 2, :],
                             ("rstd", i % 2), 6 + (i % 2),
                             [(gb_pre, lambda c, tid=tid, t0=t0: (xn[:, c, t0 - g0:t0 - g0 + TT], ("xn", c, tid)), "dve")])
            blocks = [[(jj * 256, 256), (DFF + jj * 256, 256)] for jj in range(FC // 2)]
            sgr = [0]

            def consume(bi, gi, tid, W, banks):
                j = bi * 2 + gi
                s = sgr[0] % 2
                sgr[0] += 1
                (pg, kg), (pu, ku) = banks
                pb.op("act", [kg], [("sg", s)], lambda e: e.activation(out=sg[:, s, :], in_=pg, func=AF.Silu))
                lo = tid * TT - g0
                pb.op("dve", [("sg", s), ku], [("a", j, tid)],
                      lambda e: e.tensor_tensor(out=a_buf[:, j, lo:lo + TT], in0=pu, in1=sg[:, s, :], op=ALU.mult))
            mm_stage(pb, cx, w_gu, KC, blocks, [[0, 2], [1, 3]], tiles,
                     lambda k, tid: (xn[:, k, tid * TT - g0:tid * TT - g0 + TT], ("xn", k, tid)), consume,
                     [[0, 1], [2, 3], [4, 5]])
            prefetch_w(pb, cx, w_dn, FC, [(0, 256)])
        pb.barrier()
        with sbt(nc, "ysb", [128, KC, G], F32) as ysb, sbt(nc, "sq2", [128, 2, KC, TT], BF16) as sq2, \
                sbt(nc, "rstd2", [128, 2, TT], F32) as rstd2, sbt(nc, "tmp", [128, 2, TT], F32) as tmp:
            blocks = [[(jj * 256, 256)] for jj in range(4)]

            def consume2(bi, gi, tid, W, banks):
                oc = bi * 2 + gi
                lo = tid * TT - g0
                ti = (tid - g0 // TT) % 2
                (p, k), = banks
                evac_y(pb, cx, p, k, W, ysb[:, oc, lo:lo + TT], ("y", oc, tid), sq2[:, ti, oc, :], ("sq2", ti, oc), gcol(cx, gb_post, oc))
            mm_stage(pb, cx, w_dn, FC, blocks, [[0], [1]], tiles,
                     lambda k, tid: (a_buf[:, k, tid * TT - g0:tid * TT - g0 + TT], ("a", k, tid)), consume2,
                     [[0], [1], [2], [3]])
            for i, (tid, W) in enumerate(tiles):
                lo = tid * TT - g0
                ti = i % 2
                rstd_from_sq(pb, cx, lambda c: sq2[:, ti, c, :], [("sq2", ti, c) for c in range(KC)], KC, TT,
                             rstd2[:, ti, :], ("rstd2", ti), 6 + ti)
                for c in range(KC):
                    eng, ts_ = ("dve", 0) if c < 6 else ("pool", 1)
                    tk = ("tmp", ts_)
                    ta = tmp[:, ts_, :]
                    ya = ysb[:, c, lo:lo + TT]
                    pb.op(eng, [("y", c, tid), ("rstd2", ti)], [tk],
                          lambda e, ya=ya, ta=ta, ti=ti: e.tensor_tensor(out=ta, in0=ya, in1=rstd2[:, ti, :], op=ALU.mult))
                    ha = h[:, c, tid * TT:tid * TT + TT]
                    hk = ("h", c, tid)
                    pb.op(eng, [tk, hk], [hk], lambda e, ta=ta, ha=ha: e.tensor_tensor(out=ha, in0=ta, in1=ha, op=ALU.add))
            if nxt is not None:
                prefetch_w(pb, cx, *nxt)
        pb.barrier()


def proj_post_group(pb, cx, h, g0, w_ap, xin, gb_post, nxt=None):
    nc = pb.nc
    tiles = [(g0 // TT + i, TT) for i in range(G // TT)]
    with sbt(nc, "ysb", [128, KC, G], F32) as ysb, sbt(nc, "sq2", [128, 2, KC, TT], BF16) as sq2, \
            sbt(nc, "rstd2", [128, 2, TT], F32) as rstd2, sbt(nc, "tmp", [128, 2, TT], F32) as tmp:
        blocks = [[(jj * 256, 256)] for jj in range(4)]

        def consume2(bi, gi, tid, W, banks):
            oc = bi * 2 + gi
            lo = tid * TT - g0
            ti = (tid - g0 // TT) % 2
            (p, k), = banks
            evac_y(pb, cx, p, k, W, ysb[:, oc, lo:lo + TT], ("y", oc, tid), sq2[:, ti, oc, :], ("sq2", ti, oc), gcol(cx, gb_post, oc))
        mm_stage(pb, cx, w_ap, KC, blocks, [[0], [1]], tiles, xin, consume2, [[0], [1], [2], [3]])
        for i, (tid, W) in enumerate(tiles):
            lo = tid * TT - g0
            ti = i % 2
            rstd_from_sq(pb, cx, lambda c: sq2[:, ti, c, :], [("sq2", ti, c) for c in range(KC)], KC, TT,
                         rstd2[:, ti, :], ("rstd2", ti), 6 + ti)
            for c in range(KC):
                eng, ts_ = ("dve", 0) if c < 6 else ("pool", 1)
                tk = ("tmp", ts_)
                ta = tmp[:, ts_, :]
                ya = ysb[:, c, lo:lo + TT]
                pb.op(eng, [("y", c, tid), ("rstd2", ti)], [tk],
                      lambda e, ya=ya, ta=ta, ti=ti: e.tensor_tensor(out=ta, in0=ya, in1=rstd2[:, ti, :], op=ALU.mult))
                ha = h[:, c, tid * TT:tid * TT + TT]
                hk = ("h", c, tid)
                pb.op(eng, [tk, hk], [hk], lambda e, ta=ta, ha=ha: e.tensor_tensor(out=ha, in0=ta, in1=ha, op=ALU.add))
        if nxt is not None:
            prefetch_w(pb, cx, *nxt)
    pb.barrier()


def conv_mixer_group(pb, cx, h, g0, xh, w_in, uh, first):
    raise NotImplementedError


def layer0_group(pb, cx, h, g0, xh, uh, W_, first, nxt=None):
    nc = pb.nc
    tiles = [(g0 // TT + i, TT) for i in range(G // TT)]
    with sbt(nc, "z_buf", [128, KC, G], BF16) as z_buf:
        with sbt(nc, "xn", [128, KC, G], BF16) as xn, sbt(nc, "sq", [128, KC, TT], BF16) as sqb, \
                sbt(nc, "rstd", [128, 2, TT], F32) as rstd, sbt(nc, "hs", [128, 2, TT], F32) as hs, \
                sbt(nc, "u", [128, 2, TT + 2], F32) as ub, sbt(nc, "cv", [128, 2, TT], F32) as cv, \
                sbt(nc, "xnh", [128, KC, 2], BF16) as xnh:
            if first:
                prenorm_tile(pb, cx, lambda c: (xh[:, c, :], ("xh", c)), 2, sqb, rstd[:, 0, 0:2], ("rstd", 0), 6,
                             [(C_MIXPRE0, lambda c: (xnh[:, c, :], ("xnh", c)), "dve")])
            for i, (tid, W) in enumerate(tiles):
                t0 = tid * TT
                prenorm_tile(pb, cx, lambda c: (h[:, c, t0:t0 + W], ("h", c, tid)), W, sqb, rstd[:, i % 2, :],
                             ("rstd", i % 2), 6 + (i % 2),
                             [(C_MIXPRE0, lambda c, tid=tid, t0=t0: (xn[:, c, t0 - g0:t0 - g0 + TT], ("xn", c, tid)), "dve")])
            blocks = [[(jj * 256, 256), (D + jj * 256, 256), (2 * D + jj * 256, 256)] for jj in range(4)]
            rr = [0]

            def consume(bi, gi, tid, W, banks):
                j = bi * 2 + gi
                s = rr[0] % 2
                rr[0] += 1
                (pbk, kb), (pck, kc_), (phk, kh) = banks
                if tid < 0:
                    pb.op("act", [kh], [("hs", s)], lambda e: e.activation(out=hs[:, s, 0:2], in_=phk, func=AF.Copy))
                    pb.op("dve", [("hs", s), kc_], [("uh", j)],
                          lambda e: e.tensor_tensor(out=uh[:, j, :], in0=pck, in1=hs[:, s, 0:2], op=ALU.mult))
                    return
                pb.op("act", [kh], [("hs", s)], lambda e: e.activation(out=hs[:, s, :], in_=phk, func=AF.Copy))
                pb.op("dve", [("uh", j)], [("u", s)], lambda e: e.tensor_copy(out=ub[:, s, 0:2], in_=uh[:, j, :]))
                pb.op("dve", [("hs", s), kc_, ("u", s)], [("u", s)],
                      lambda e: e.tensor_tensor(out=ub[:, s, 2:TT + 2], in0=pck, in1=hs[:, s, :], op=ALU.mult))
                pb.op("dve", [("u", s)], [("uh", j)], lambda e: e.tensor_copy(out=uh[:, j, :], in_=ub[:, s, TT:TT + 2]))
                cw = lambda t: cx.consts[:, C_CONV + t * 8 + j:C_CONV + t * 8 + j + 1]
                pb.op("dve", [("u", s), "consts"], [("cv", s)],
                      lambda e: e.tensor_scalar(out=cv[:, s, :], in0=ub[:, s, 2:TT + 2], scalar1=cw(2), scalar2=None, op0=ALU.mult))
                pb.op("dve", [("u", s), ("cv", s), "consts"], [("cv", s)],
                      lambda e: e.scalar_tensor_tensor(out=cv[:, s, :], in0=ub[:, s, 1:TT + 1], scalar=cw(1), in1=cv[:, s, :],
                                                       op0=ALU.mult, op1=ALU.add))
                pb.op("dve", [("u", s), ("cv", s), "consts"], [("cv", s)],
                      lambda e: e.scalar_tensor_tensor(out=cv[:, s, :], in0=ub[:, s, 0:TT], scalar=cw(0), in1=cv[:, s, :],
                                                       op0=ALU.mult, op1=ALU.add))
                lo = tid * TT - g0
                pb.op("dve", [("cv", s), kb], [("z", j, tid)],
                      lambda e: e.tensor_tensor(out=z_buf[:, j, lo:lo + TT], in0=pbk, in1=cv[:, s, :], op=ALU.mult))

            def xin(k, tid):
                if tid < 0:
                    return (xnh[:, k, :], ("xnh", k))
                return (xn[:, k, tid * TT - g0:tid * TT - g0 + TT], ("xn", k, tid))
            tl = ([(-1, 2)] if first else []) + tiles
            mm_stage(pb, cx, W_["conv_w_in"], KC, blocks, [[0, 2, 4], [1, 3, 5]], tl, xin, consume,
                     [[0, 1, 2], [3, 4, 5]])
            prefetch_w(pb, cx, W_["conv_w_out"], KC, [(0, 256)])
        pb.barrier()
        proj_post_group(pb, cx, h, g0, W_["conv_w_out"],
                        lambda k, tid: (z_buf[:, k, tid * TT - g0:tid * TT - g0 + TT], ("z", k, tid)), C_MIXPOST0,
                        nxt=(W_["gu0"], KC, [(0, 256), (DFF, 256)]))
    ffn_group(pb, cx, h, g0, 0, W_["gu0"], W_["dn0"], C_FFNPRE0, C_FFNPOST0, nxt=nxt)


def rope_tables(pb, cx, pos_d, cosT, sinT):
    nc = pb.nc
    C1 = 6.28125
    C2 = float(2.0 * np.pi - 6.28125)
    with sbt(nc, "posi", [128, TOWN], I32) as posi, sbt(nc, "ang", [128, TOWN], F32) as ang, \
            sbt(nc, "red", [128, TOWN], F32) as red, sbt(nc, "kf", [128, TOWN], F32) as kf, \
            sbt(nc, "ki", [128, TOWN], I32) as ki:
        pb.dma("sp", posi[:], pos_d, [], ["posi"], "posi")
        pb.op("dve", ["posi"], ["ang"], lambda e: e.tensor_copy(out=ang[:], in_=posi[:]))
        pb.op("dve", ["ang", "consts"], ["ang"],
              lambda e: e.tensor_scalar(out=ang[:], in0=ang[:], scalar1=cx.consts[:, C_INVF:C_INVF + 1], scalar2=None, op0=ALU.mult))

        def sin_of(dst, dkey, shift):
            pb.op("dve", ["ang"], ["kf"],
                  lambda e: e.tensor_scalar(out=kf[:], in0=ang[:], scalar1=float(shift), scalar2=float(1.0 / (2.0 * np.pi)),
                                            op0=ALU.add, op1=ALU.mult))
            pb.op("dve", ["kf"], ["ki"], lambda e: e.tensor_copy(out=ki[:], in_=kf[:]))
            pb.op("dve", ["ki"], ["kf"], lambda e: e.tensor_copy(out=kf[:], in_=ki[:]))
            pb.op("dve", ["ang"], ["red"],
                  lambda e: e.tensor_scalar(out=red[:], in0=ang[:], scalar1=float(shift), scalar2=None, op0=ALU.add))
            pb.op("dve", ["kf", "red"], ["red"],
                  lambda e: e.scalar_tensor_tensor(out=red[:], in0=kf[:], scalar=-C1, in1=red[:], op0=ALU.mult, op1=ALU.add))
            pb.op("dve", ["kf", "red"], ["red"],
                  lambda e: e.scalar_tensor_tensor(out=red[:], in0=kf[:], scalar=-C2, in1=red[:], op0=ALU.mult, op1=ALU.add))
            pb.op("dve", ["red"], ["kf"],
                  lambda e: e.tensor_scalar(out=kf[:], in0=red[:], scalar1=float(np.pi), scalar2=float(2.0 * np.pi),
                                            op0=ALU.is_gt, op1=ALU.mult))
            pb.op("dve", ["red", "kf"], ["red"], lambda e: e.tensor_tensor(out=red[:], in0=red[:], in1=kf[:], op=ALU.subtract))
            pb.op("dve", ["red"], ["kf"],
                  lambda e: e.tensor_scalar(out=kf[:], in0=red[:], scalar1=float(-np.pi), scalar2=float(2.0 * np.pi),
                                            op0=ALU.is_lt, op1=ALU.mult))
            pb.op("dve", ["red", "kf"], ["red"], lambda e: e.tensor_tensor(out=red[:], in0=red[:], in1=kf[:], op=ALU.add))
            pb.op("dve", ["red"], ["red"],
                  lambda e: e.tensor_scalar(out=red[:], in0=red[:], scalar1=3.1415925, scalar2=-3.1415925, op0=ALU.min, op1=ALU.max))
            pb.op("act", ["red"], [dkey], lambda e: e.activation(out=dst[:], in_=red[:], func=AF.Sin))
        sin_of(sinT, "sinT", 0.0)
        sin_of(cosT, "cosT", float(np.pi / 2))
        pb.barrier()


def res_view(buf_ap_2d, d, t0, W):
    v = buf_ap_2d.rearrange("p (r j) -> p j r", r=d)
    return v[:, t0 // d:(t0 + W) // d, :]


def kvq_phase(pb, cx, h, W_, pos_d, KT_d, V_d, QT_d, halo=False, after_prenorm=None, nxt=None):
    nc = pb.nc
    ntile = TOWN // TT
    with sbt(nc, "cosT", [128, TOWN], F32) as cosT, sbt(nc, "sinT", [128, TOWN], F32) as sinT:
        rope_tables(pb, cx, pos_d, cosT, sinT)
        with sbt(nc, "kn", [128, KC, TOWN], BF16) as kn, sbt(nc, "sq", [128, KC, TT], BF16) as sqb, \
                sbt(nc, "rstd", [128, 4, TT], F32) as rstd, \
                sbt(nc, "kst", [128, 1, 2, 2, TOWN], BF16) as kst, \
                sbt(nc, "vst", [128, 2, 4, 512], BF16) as vst, \
                sbt(nc, "rtmp", [128, 2, 4, TT], F32) as rtmp:
            for which in (("kv",) if halo else ("kv", "q")):
                gb = C_KVN if which == "kv" else C_MIXPRE1
                for i in range(ntile):
                    t0 = i * TT
                    if which == "kv":
                        prenorm_tile(pb, cx, lambda c: (h[:, c, t0:t0 + TT], ("h", c, i)), TT, sqb, rstd[:, i, :],
                                     ("rstd", i), 6 + (i % 2),
                                     [(gb, lambda c, i=i, t0=t0: (kn[:, c, t0:t0 + TT], ("kn", c, i)), "dve")])
                    else:
                        for c in range(KC):
                            pb.op("dve", [("h", c, i), ("rstd", i), "consts"], [("kn", c, i)],
                                  lambda e, c=c, i=i, t0=t0: e.scalar_tensor_tensor(
                                      out=kn[:, c, t0:t0 + TT], in0=h[:, c, t0:t0 + TT], scalar=gcol(cx, gb, c), in1=rstd[:, i, :],
                                      op0=ALU.mult, op1=ALU.mult))
                if which == "kv" and after_prenorm is not None:
                    after_prenorm()
                w_ap = W_["w_kv"] if which == "kv" else W_["w_q"]
                dst_d = KT_d if which == "kv" else QT_d
                rs = [0]

                def consume(bi, gi, tid, W, banks, which=which, dst_d=dst_d, meta=None, part=False):
                    g, mm, par = meta[bi]
                    m = 2 * mm + gi
                    d = BRANCH_D[g]
                    s = rs[0] % 2
                    rs[0] += 1
                    (pA, kA), (pB, kB) = banks
                    t0 = tid * TT
                    cs_ = cosT[:, t0:t0 + TT]
                    sn_ = sinT[:, t0:t0 + TT]
                    for idx, (src, sk, tab, tk) in enumerate(((pA, kA, cs_, "cosT"), (pB, kB, sn_, "sinT"),
                                                               (pB, kB, cs_, "cosT"), (pA, kA, sn_, "sinT"))):
                        pb.op("dve", [sk, tk], [("rtmp", s, idx)],
                              lambda e, src=src, tab=tab, idx=idx: e.tensor_tensor(out=rtmp[:, s, idx, :], in0=src, in1=tab, op=ALU.mult))
                    for ab, (i_a, i_b, op_, eng) in enumerate(((0, 1, ALU.subtract, "dve"), (2, 3, ALU.add, "pool"))):
                        if part:
                            dv = kst[:, par, gi, ab, 0:TT].rearrange("p (r j) -> p r j", r=d)
                        else:
                            dv = kst[:, par, gi, ab, :].rearrange("p (r j) -> p r j", r=d)[:, :, t0 // d:(t0 + TT) // d]
                        kkeys = [("kst", par, gi, ab, t) for t in range(ntile)] if part else [("kst", par, gi, ab, tid)]
                        pb.op(eng, [("rtmp", s, i_a), ("rtmp", s, i_b)], kkeys,
                              lambda e, dv=dv, i_a=i_a, i_b=i_b, op_=op_: e.tensor_tensor(
                                  out=dv, in0=rtmp[:, s, i_a, :].rearrange("p (j r) -> p r j", r=d),
                                  in1=rtmp[:, s, i_b, :].rearrange("p (j r) -> p r j", r=d), op=op_))
                    if tid == ntile - 1:
                        for ab in range(2):
                            cc = 4 * ab + m
                            if part:
                                lo = TT - 128 * d
                                pb.dma("sp", dst_d[g, cc, :, 0:128 * d], kst[:, par, gi, ab, lo:TT],
                                       [("kst", par, gi, ab, t) for t in range(ntile)], [(which, g, cc)], "kst%d%d%d" % (par, gi, ab))
                            else:
                                pb.dma("sp", dst_d[g, cc], kst[:, par, gi, ab, :], [("kst", par, gi, ab, t) for t in range(ntile)],
                                       [(which, g, cc)], "kst%d%d%d" % (par, gi, ab))
                xin = lambda k, tid: (kn[:, k, tid * TT:tid * TT + TT], ("kn", k, tid))

                def kq_blocks(gl):
                    bl, meta = [], []
                    for g in gl:
                        for mm in range(2):
                            bl.append([(g * D + mm * 256, 256), (g * D + 512 + mm * 256, 256)])
                            meta.append((g, mm, 0))
                            cx.kst_rr += 1
                    return bl, meta
                if halo:
                    bl, meta = kq_blocks([0, 1])
                    mm_stage(pb, cx, w_ap, KC, bl, [[0, 2], [1, 3]], [(ntile - 1, TT)], xin,
                             lambda bi, gi, tid, W, banks, meta=meta: consume(bi, gi, tid, W, banks, meta=meta, part=True),
                             [[0, 1], [2, 3], [4, 5]])
                    bl, meta = kq_blocks([2])
                    mm_stage(pb, cx, w_ap, KC, bl, [[0, 2], [1, 3]], [(i, TT) for i in range(ntile)], xin,
                             lambda bi, gi, tid, W, banks, meta=meta: consume(bi, gi, tid, W, banks, meta=meta, part=False),
                             [[0, 1], [2, 3], [4, 5]])
                else:
                    bl, meta = kq_blocks([0, 1, 2])
                    mm_stage(pb, cx, w_ap, KC, bl, [[0, 2], [1, 3]], [(i, TT) for i in range(ntile)], xin,
                             lambda bi, gi, tid, W, banks, meta=meta: consume(bi, gi, tid, W, banks, meta=meta, part=False),
                             [[0, 1], [2, 3], [4, 5]])
                if which == "kv":
                    vi = [0]
                    vjobs = [(g, half) for g in range(3) for half in range(2)]
                    vahead = []
                    vreq = [0]

                    def v_request(upto):
                        while vreq[0] < len(vjobs) and vreq[0] <= upto:
                            g_, half_ = vjobs[vreq[0]]
                            vahead.append(load_w(pb, cx, w_ap, KC, [(3 * D + g_ * D + half_ * 512, 512)]))
                            vreq[0] += 1
                    for g in range(3):
                        d = BRANCH_D[g]
                        nb = TOWN // (128 * d)
                        if halo:
                            vblocks = [(r, r, nb - 1) for r in range(d)]
                        else:
                            vblocks = [(blk, blk // nb, blk % nb) for blk in range(16)]
                        for half in range(2):
                            v_request(2 * g + half + NWSLOT - 2)
                            view, wkeys = vahead.pop(0)
                            pend = []
                            for (di, r, n) in vblocks:
                                bank = vi[0] % 4
                                vi[0] += 1
                                base = n * 128 * d
                                if not pend:
                                    vslot = cx.vrr % 2
                                    cx.vrr += 1

                                def fn(e, base=base, d=d, bank=bank, view=view, r=r):
                                    ins = None
                                    for kk in range(KC):
                                        lhsT = kn[:, kk, base:base + 128 * d].rearrange("p (i r) -> p i r", r=d)[:, :, r]
                                        ins = e.matmul(cx.ps[:, bank, :], lhsT=lhsT, rhs=view[:, kk, :],
                                                       start=(kk == 0), stop=(kk == KC - 1))
                                    return ins
                                pb.op("pe", list(wkeys) + [("kn", kk, t) for kk in range(KC) for t in range(ntile)],
                                      [("ps", bank)], fn)
                                vs_i = len(pend)
                                pb.op("act", [("ps", bank)], [("vst", vslot, vs_i)],
                                      lambda e, bank=bank, vslot=vslot, vs_i=vs_i: e.activation(
                                          out=vst[:, vslot, vs_i, :], in_=cx.ps[:, bank, :], func=AF.Copy))
                                pend.append(di)
                                if len(pend) == 4 or di == vblocks[-1][0]:
                                    b0 = pend[0]
                                    nn = len(pend)
                                    pb.dma("sp", V_d[g, :, b0:b0 + nn, half * 512:half * 512 + 512], vst[:, vslot, 0:nn, :],
                                           [("vst", vslot, x) for x in range(nn)], [("V", g, half, b0 // 4)], "vst%d" % vslot)
                                    pend = []
                if nxt is not None and which == ("kv" if halo else "q"):
                    prefetch_w(pb, cx, *nxt)
                pb.barrier()


def attention_phase(pb, cx, o_all, QT_d, KT_d, KTh_d, V_d, Vh_d, masks_d, nxt=None):
    nc = pb.nc
    with sbt(nc, "masks", [128, 4, 512], BF16) as masks, sbt(nc, "ident", [128, 128], BF16) as ident, \
            sbt(nc, "accn", [128, TOWN], F32) as accn, sbt(nc, "accd", [128, TOWN], F32) as accd, \
            sbt(nc, "qb", [128, 2, TOWN], BF16) as qb, sbt(nc, "kb", [128, 2, 2 * TOWN], BF16) as kb, \
            sbt(nc, "vb", [128, 2, 32, 192], BF16) as vb, sbt(nc, "pT", [128, 4, 512], BF16) as pT:
        for uu in range(2):
            pb.op("dve", [], [("vones", uu)], lambda e, uu=uu: e.memset(vb[:, uu, :, 64:128], 1.0))
        pb.dma("sp", masks[:], masks_d[0], [], ["masks"], "masks")
        pb.dma("sp", ident[:], masks_d[1][:, 0, 0:128], [], ["ident"], "masks")
        M_HPPP, M_PPPP, M_HHHH, M_CCCC = 0, 1, 2, 3
        ui = 0
        sc_rr = 0
        nd_rr = 0
        pt_rr = 0
        pending = []
        LAG = 2

        def flush():
            while pending:
                pending.pop(0)[1]()

        def release():
            while pending and (pending[0][0] == "evac" or sum(1 for kd, _ in pending if kd == "pv") > LAG):
                pending.pop(0)[1]()
        for c in range(KC):
            for g in DBG_BRANCHES:
                d = BRANCH_D[g]
                nb = TOWN // (128 * d)
                u = ui % 2
                ui += 1
                HOFF = TOWN
                mq, eo = c // 2, 64 * (c % 2)
                qkeys = []
                for hh in range(2):
                    for ab in range(2):
                        prt = slice(64 * hh + 32 * ab, 64 * hh + 32 * ab + 32)
                        rws = slice(eo + 32 * hh, eo + 32 * hh + 32)
                        cc = 4 * ab + mq
                        pb.dma("sp", qb[prt, u, :], QT_d[g, cc, rws, :], [("q", g, cc)], [("qb", u, hh, ab)], "qb%d" % u)
                        pb.dma("sp", kb[prt, u, HOFF:HOFF + TOWN], KT_d[g, cc, rws, :], [("kv", g, cc)], [("kb", u, hh, ab)], "kb%d" % u)
                        pb.dma("sp", kb[prt, u, 0:128 * d], KTh_d[g, cc, rws, 0:128 * d], [], [("kbh", u, hh, ab)], "kbh%d" % u)
                        qkeys += [("qb", u, hh, ab), ("kb", u, hh, ab), ("kbh", u, hh, ab)]
                for hh in range(2):
                    vc = slice(128 * hh, 128 * hh + 64)
                    sc_ = slice(c * 128 + 64 * hh, c * 128 + 64 * hh + 64)
                    pb.dma("sp", vb[:, u, 16:32, vc], V_d[g, :, :, sc_],
                           [("V", g, hf, bb) for hf in range(2) for bb in range(4)], [("vb", u, hh)], "vb%d" % u)
                    pb.dma("sp", vb[:, u, 0:d, vc], Vh_d[g, :, 0:d, sc_], [], [("vbh", u, hh)], "vbh%d" % u)
                for sb in range(4):
                    nbank, dbank = (4, 5) if nd_rr % 2 == 0 else (6, 7)
                    nd_rr += 1
                    for e_ in range(2):
                        pl = slice(64 * e_, 64 * e_ + 64)
                        for side in range(2):
                            sbank = sc_rr % 4
                            sc_rr += 1
                            ps_s = cx.ps[:, sbank, :]
                            blks = [4 * sb + x for x in range(4)]
                            if side == 1:
                                mk = M_CCCC
                            else:
                                fl = [(bq % nb) == 0 for bq in blks]
                                mk = M_HHHH if all(fl) else (M_HPPP if fl[0] else M_PPPP)

                            def kview(bq):
                                r, n = bq // nb, bq % nb
                                if side == 1:
                                    o = HOFF + bq * 128
                                elif n == 0:
                                    o = r * 128
                                else:
                                    o = HOFF + (bq - 1) * 128
                                return o

                            def vidx(bq):
                                r, n = bq // nb, bq % nb
                                if side == 1:
                                    return 16 + bq
                                elif n == 0:
                                    return r
                                return 16 + bq - 1

                            def fscore(e, ps_s=ps_s, mk=mk, kview=kview, blks=blks, pl=pl, e_=e_, u=u):
                                e.matmul(ps_s, lhsT=ident[:], rhs=masks[:, mk, :], start=True, stop=False)
                                ins = None
                                for x, bq in enumerate(blks):
                                    o = kview(bq)
                                    ins = e.matmul(ps_s[:, x * 128:(x + 1) * 128], lhsT=kb[pl, u, o:o + 128],
                                                   rhs=qb[pl, u, bq * 128:(bq + 1) * 128], start=False, stop=(x == 3),
                                                   tile_position=(64 * e_, 0))
                                return ins
                            pb.op("pe", qkeys + ["masks", "ident"], [("ps", sbank)], fscore)
                            pslot = pt_rr % 4
                            pt_rr += 1
                            pb.op("act", [("ps", sbank)], [("pT", pslot)],
                                  lambda e, ps_s=ps_s, pslot=pslot: e.activation(out=pT[:, pslot, :], in_=ps_s, func=AF.Exp,
                                                                                 scale=0.125))

                            hbank = nbank if e_ == 0 else dbank
                            vlist = [vidx(bq) for bq in blks]

                            def do_pv(vlist=vlist, pslot=pslot, e_=e_, side=side, u=u, hbank=hbank):
                                def fpv(e):
                                    ins = None
                                    for x in range(4):
                                        cs = slice(x * 128, (x + 1) * 128)
                                        ins = e.matmul(cx.ps[:, hbank, cs], lhsT=vb[:, u, vlist[x], 64 * e_:64 * e_ + 128],
                                                       rhs=pT[:, pslot, cs], start=(side == 0 and x == 0), stop=(side == 1 and x == 3))
                                    return ins
                                pb.op("pe", [("pT", pslot), ("vb", u, 0), ("vb", u, 1), ("vbh", u, 0), ("vbh", u, 1), ("vones", u)],
                                      [("ps", hbank)], fpv)
                            pending.append(("pv", do_pv))
                            release()
                    def do_evac(d=d, sb=sb, g=g, nbank=nbank, dbank=dbank):
                        def tview(acc, pr):
                            if d == 1:
                                return acc[pr, sb * 512:(sb + 1) * 512]
                            if d == 4:
                                return acc[pr, :].rearrange("p (j r) -> p r j", r=4)[:, sb, :]
                            return acc[pr, :].rearrange("p (j r) -> p r j", r=16)[:, 4 * sb:4 * sb + 4, :]

                        def sview(bank, pr):
                            if d == 16:
                                return cx.ps[pr, bank, :].rearrange("p (x j) -> p x j", x=4)
                            return cx.ps[pr, bank, :]
                        lo_, hi_ = slice(0, 64), slice(64, 128)
                        moves = ((accn, "accn", lo_, nbank, lo_), (accn, "accn", hi_, dbank, hi_),
                                 (accd, "accd", lo_, nbank, hi_), (accd, "accd", hi_, dbank, lo_))
                        for (acc, an, pr_o, bank, pr_i) in moves:
                            vo = tview(acc, pr_o)
                            si = sview(bank, pr_i)
                            half = 0 if pr_o is lo_ else 1
                            if g == DBG_BRANCHES[0]:
                                pb.op("dve", [("ps", bank)], [(an, sb, half)], lambda e, vo=vo, si=si: e.tensor_copy(out=vo, in_=si))
                            else:
                                kk_ = [(an, x, half) for x in range(4)]
                                pb.op("dve", [("ps", bank)] + kk_, kk_,
                                      lambda e, vo=vo, si=si: e.tensor_tensor(out=vo, in0=si, in1=vo, op=ALU.add))
                    pending.append(("evac", do_evac))
            flush()
            for t in range(TOWN // TT):
                cs = slice(t * TT, (t + 1) * TT)
                dk = [("accd", t, 0), ("accd", t, 1)]
                ak = [("accn", t, 0), ("accn", t, 1)]
                pb.op("act", dk, dk, lambda e, cs=cs: e.activation(out=accd[:, cs], in_=accd[:, cs], func=AF.Ln))
                pb.op("act", dk, dk, lambda e, cs=cs: e.activation(out=accd[:, cs], in_=accd[:, cs], func=AF.Exp, scale=-1.0))
                pb.op("pool", ak + dk, [("o", c, t)],
                      lambda e, c=c, cs=cs: e.tensor_tensor(out=o_all[:, c, cs], in0=accn[:, cs], in1=accd[:, cs], op=ALU.mult))
        if nxt is not None:
            prefetch_w(pb, cx, *nxt)
        pb.barrier()


def declare_weights(nc, names):
    shapes = dict(conv_w_in=[D, 3 * D], conv_w_out=[D, D], gu0=[D, 2 * DFF], dn0=[DFF, D], w_kv=[D, 6 * D], w_q=[D, 3 * D],
                  w_o=[D, D], gu1=[D, 2 * DFF], dn1=[DFF, D])
    return {n: nc.dram_tensor(n, shapes[n], F32, kind="ExternalInput").ap() for n in names}


def build_fused():
    nc = bass.Bass("TRN2", target_bir_lowering=False)
    xT = nc.dram_tensor("xT", [128, KC, 2 * TOWN], F32, kind="ExternalInput").ap()
    pos = nc.dram_tensor("pos", [128, 2 * TOWN], I32, kind="ExternalInput").ap()
    consts_d = nc.dram_tensor("consts", [128, NCONST], F32, kind="ExternalInput").ap()
    masks_d = nc.dram_tensor("masks", [2, 128, 4, 512], BF16, kind="ExternalInput").ap()
    W_ = declare_weights(nc, ["conv_w_in", "conv_w_out", "gu0", "dn0", "w_kv", "w_q", "w_o", "gu1", "dn1"])
    outT = nc.dram_tensor("outT", [128, KC, TOWN], F32, kind="ExternalOutput").ap()
    KT = nc.dram_tensor("KT_s", [3, KC, 128, TOWN], BF16).ap()
    QT = nc.dram_tensor("QT_s", [3, KC, 128, TOWN], BF16).ap()
    Vs = nc.dram_tensor("Vs_s", [3, 128, 16, D], BF16).ap()
    KTh = nc.dram_tensor("KTh_s", [3, KC, 128, TOWN], BF16).ap()
    Vh = nc.dram_tensor("Vh_s", [3, 128, 16, D], BF16).ap()
    pb = PB(nc)
    cx = Ctx()
    _consts_setup(pb, cx, consts_d)
    h = nc.alloc_sbuf_tensor("h_res", [128, KC, TOWN], F32)
    uh = nc.alloc_sbuf_tensor("uh", [128, KC, 2], F32)
    pb.op("dve", [], [("uh", j) for j in range(KC)], lambda e: e.memset(uh[:], 0.0))
    def load_x(part):
        for i in range(TOWN // TT):
            pb.dma("sp", h[:, :, i * TT:(i + 1) * TT], xT[:, :, part * TOWN + i * TT:part * TOWN + (i + 1) * TT], [],
                   [("h", c, i) for c in range(KC)], "hin%d" % i)
    load_x(0)
    for part in range(2):
        w_in0 = (W_["conv_w_in"], KC, [(0, 256), (D, 256), (2 * D, 256)])
        w_k0 = (W_["w_kv"], KC, [(0, 256), (512, 256)])
        for gi in range(TOWN // G):
            layer0_group(pb, cx, h, gi * G, None, uh, W_, False, nxt=(w_in0 if gi == 0 else w_k0))
        if part == 0:
            kvq_phase(pb, cx, h, W_, pos[:, 0:TOWN], KTh, Vh, None, halo=True, after_prenorm=lambda: load_x(1), nxt=w_in0)
        else:
            kvq_phase(pb, cx, h, W_, pos[:, TOWN:2 * TOWN], KT, Vs, QT, halo=False)
        if STOP_AFTER == part + 1:
            for i in range(TOWN // TT):
                pb.dma("sp", outT[:, :, i * TT:(i + 1) * TT], h[:, :, i * TT:(i + 1) * TT], [("h", c, i) for c in range(KC)], [], "hout%d" % i)
            pb.barrier()
            return nc
    with sbt(nc, "o_all", [128, KC, TOWN], BF16) as o_all:
        attention_phase(pb, cx, o_all, QT, KT, KTh, Vs, Vh, masks_d, nxt=(W_["w_o"], KC, [(0, 256)]))
        if STOP_AFTER == 3:
            for i in range(TOWN // TT):
                pb.dma("sp", outT[:, :, i * TT:(i + 1) * TT], h[:, :, i * TT:(i + 1) * TT], [("h", c, i) for c in range(KC)], [], "hout%d" % i)
            pb.barrier()
            return nc
        for gi in range(TOWN // G):
            proj_post_group(pb, cx, h, gi * G, W_["w_o"],
                            lambda k, tid: (o_all[:, k, tid * TT:tid * TT + TT], ("o", k, tid)), C_MIXPOST1,
                            nxt=((W_["w_o"], KC, [(0, 256)]) if gi == 0 else (W_["gu1"], KC, [(0, 256), (DFF, 256)])))
    for gi in range(TOWN // G):
        ffn_group(pb, cx, h, gi * G, 1, W_["gu1"], W_["dn1"], C_FFNPRE1, C_FFNPOST1,
                  nxt=((W_["gu1"], KC, [(0, 256), (DFF, 256)]) if gi == 0 else None))
        for i in range(gi * G // TT, (gi + 1) * G // TT):
            pb.dma("sp", outT[:, :, i * TT:(i + 1) * TT], h[:, :, i * TT:(i + 1) * TT], [("h", c, i) for c in range(KC)], [], "hout%d" % i)
    pb.barrier()
    return nc


def _fm(v):
    return np.ascontiguousarray(np.asarray(v, np.float32).reshape(KC, 128).T)


def make_consts(inp):
    c = np.zeros((128, NCONST), np.float32)
    for l in range(2):
        c[:, 32 * l + 0:32 * l + 8] = _fm(inp["mix_norm_pre"][l])
        c[:, 32 * l + 8:32 * l + 16] = _fm(inp["mix_norm_post"][l])
        c[:, 32 * l + 16:32 * l + 24] = _fm(inp["ffn_norm_pre"][l])
        c[:, 32 * l + 24:32 * l + 32] = _fm(inp["ffn_norm_post"][l])
    c[:, C_KVN:C_KVN + 8] = _fm(inp["kv_norm"])
    for t in range(3):
        c[:, C_CONV + 8 * t:C_CONV + 8 * t + 8] = _fm(inp["conv_w"][0, t])
    half = 32
    inv_freq = (10000.0 ** (-np.arange(half, dtype=np.float32) / half)).astype(np.float32)
    p = np.arange(128)
    c[:, C_INVF] = inv_freq[p % 32]
    c[:, C_NSGN] = np.where((p // 32) % 2 == 0, 1.0, -1.0)
    c[:, C_EPS] = EPS
    c[:, C_NEG1] = -1.0
    return c


def rope_perm():
    perm = np.zeros(3 * D, np.int64)
    for g in range(3):
        for cc in range(8):
            ab, m = cc // 4, cc % 4
            for p in range(128):
                perm[g * D + cc * 128 + p] = g * D + (4 * m + p // 32) * 64 + 32 * ab + (p % 32)
    return perm


def make_masks(rank):
    i = np.arange(128)
    kj, qi = i[:, None], i[None, :]
    m_prev = np.where(kj >= qi, 0.0, NEG).astype(np.float32)
    m_cur = np.where(kj <= qi, 0.0, NEG).astype(np.float32)
    m_halo = m_prev if rank == 1 else np.full((128, 128), NEG, np.float32)
    def cat(*ms):
        return np.concatenate(ms, axis=1)
    m = np.stack([cat(m_halo, m_prev, m_prev, m_prev), cat(m_prev, m_prev, m_prev, m_prev),
                  cat(m_halo, m_halo, m_halo, m_halo), cat(m_cur, m_cur, m_cur, m_cur)], axis=1)
    idn = np.zeros((128, 4, 512), np.float32)
    idn[:, 0, 0:128] = np.eye(128, dtype=np.float32)
    return np.stack([m, idn], axis=0).astype(ml_dtypes.bfloat16)


_CACHE = {}


def _get(name, fn):
    if name not in _CACHE:
        _CACHE[name] = fn()
    return _CACHE[name]


def in_maps_fused(I, cores):
    inp = {k: np.asarray(I[k]) for k in ("mix_norm_pre", "mix_norm_post", "ffn_norm_pre", "ffn_norm_post", "kv_norm", "conv_w")}
    x = np.asarray(I["x"], np.float32)
    positions = np.asarray(I["positions"], np.int32)
    consts = make_consts(inp)
    f32c = lambda a: np.ascontiguousarray(np.asarray(a, np.float32))
    perm = rope_perm()
    wkv = np.asarray(I["w_kv"], np.float32)
    wkv_p = np.concatenate([wkv[:, :3 * D][:, perm], wkv[:, 3 * D:]], axis=1)
    W = dict(conv_w_in=f32c(I["conv_w_in"][0]), conv_w_out=f32c(I["conv_w_out"][0]), gu0=f32c(I["ffn_w_gate_up"][0]),
             dn0=f32c(I["ffn_w_down"][0]), w_kv=f32c(wkv_p), w_q=f32c(np.asarray(I["w_q"][0], np.float32)[:, perm]),
             w_o=f32c(I["w_o"][0]), gu1=f32c(I["ffn_w_gate_up"][1]), dn1=f32c(I["ffn_w_down"][1]))
    maps = []
    for core in cores:
        b, rank = core // 2, core % 2
        xs = np.zeros((2 * TOWN, D), np.float32)
        ps = np.zeros((2 * TOWN,), np.int32)
        if rank == 1:
            xs[:] = x[b]
            ps[:] = positions[b]
        else:
            xs[TOWN:] = x[b, 0:TOWN]
            ps[TOWN:] = positions[b, 0:TOWN]
        xT = np.ascontiguousarray(xs.T.reshape(KC, 128, 2 * TOWN).transpose(1, 0, 2))
        pos = np.ascontiguousarray(np.broadcast_to(ps[None, :], (128, 2 * TOWN)))
        m = dict(xT=xT, pos=pos, consts=consts, masks=make_masks(rank))
        m.update(W)
        maps.append(m)
    return maps


def kernel(**I):
    ncores = 8
    cores = list(range(ncores))
    res = run_bass_kernel_spmd(build_fused(), in_maps_fused(I, cores), core_ids=cores)
    out = np.zeros((4, 2 * TOWN, D), np.float32)
    for core in cores:
        b, rank = core // 2, core % 2
        oT = np.asarray(res.results[core]["outT"])
        out[b, rank * TOWN:(rank + 1) * TOWN] = oT.transpose(2, 1, 0).reshape(TOWN, D)
    return out
```

```python
import numpy as np
import ml_dtypes
import concourse.bass as bass
import concourse.mybir as mybir
from concourse.bass_utils import run_bass_kernel_spmd

F32 = mybir.dt.float32
BF16 = mybir.dt.bfloat16
I32 = mybir.dt.int32
ALU = mybir.AluOpType
AF = mybir.ActivationFunctionType

D = 1024
KC = 8
TOWN = 2048
G = 1024
TT = 512
DFF = 2816
FC = 22
BRANCH_D = (1, 4, 16)
EPS = 1e-6
WSLOT = 6144
NWSLOT = 3
NEG = -30000.0
DEBUG = False
SAME_ENG_STRICT = True
DMA_TOTAL_WAIT = True
STOP_AFTER = 0
RSTD_LNEXP = True
DBG_BRANCHES = (0, 1, 2)

C_MIXPRE0, C_MIXPOST0, C_FFNPRE0, C_FFNPOST0 = 0, 8, 16, 24
C_MIXPRE1, C_MIXPOST1, C_FFNPRE1, C_FFNPOST1 = 32, 40, 48, 56
C_KVN, C_CONV, C_INVF, C_NSGN, C_EPS, C_NEG1 = 64, 72, 96, 97, 98, 99
NCONST = 104


class PB:
    def __init__(self, nc):
        self.nc = nc
        self.E = dict(pe=nc.tensor, act=nc.scalar, dve=nc.vector, pool=nc.gpsimd, sp=nc.sync)
        self.sem = {k: nc.alloc_semaphore("s_" + k) for k in self.E}
        self.cnt = {k: 0 for k in self.E}
        self.waited = {k: {} for k in self.E}
        self.st = {}
        self.dsem = {}
        self.bank_rr = 0

    def _need(self, eng, evs):
        for ev in evs:
            name, sem, val = ev[0], ev[1], ev[2]
            if ev[3] == "dma" and DMA_TOTAL_WAIT:
                val = self.dsem[name[2:]][1]
            if self.waited[eng].get(name, 0) < val:
                self.E[eng].wait_ge(sem, val)
                self.waited[eng][name] = val

    def _hazards(self, eng, reads, writes, is_dma):
        evs = []
        for k in reads:
            s = self.st.get(k)
            if s and s[0] is not None:
                evs.append(s[0])
        strict = SAME_ENG_STRICT and eng != "pe"
        for k in writes:
            s = self.st.get(k)
            if s:
                if s[0] is not None and (is_dma or strict or s[0][3] != eng):
                    evs.append(s[0])
                for ev in s[1].values():
                    if is_dma or strict or ev[3] != eng:
                        evs.append(ev)
        return evs

    def _commit(self, ev, reads, writes):
        for k in reads:
            s = self.st.setdefault(k, [None, {}])
            s[1][ev[0]] = ev
        for k in writes:
            self.st[k] = [ev, {}]

    def op(self, eng, reads, writes, fn):
        self._need(eng, self._hazards(eng, reads, writes, False))
        ins = fn(self.E[eng])
        self.cnt[eng] += 1
        ins.then_inc(self.sem[eng], 1)
        ev = (eng, self.sem[eng], self.cnt[eng], eng)
        self._commit(ev, reads, writes)

    def dma(self, q, out, in_, reads, writes, slot):
        self._need(q, self._hazards(q, reads, writes, True))
        if slot not in self.dsem:
            self.dsem[slot] = [self.nc.alloc_semaphore("d_" + slot), 0]
        d = self.dsem[slot]
        self.E[q].dma_start(out=out, in_=in_).then_inc(d[0], 16)
        d[1] += 16
        ev = ("d_" + slot, d[0], d[1], "dma")
        self._commit(ev, reads, writes)

    def barrier(self):
        evs = [(k, self.sem[k], self.cnt[k], k) for k in self.E if self.cnt[k] > 0]
        evs += [("d_" + s, d[0], d[1], "dma") for s, d in self.dsem.items() if d[1] > 0]
        for e in self.E:
            self._need(e, evs)
        self.st = {}


class Ctx:
    pass


_UNIQ = [0]


def sbt(nc, name, shape, dtype):
    _UNIQ[0] += 1
    return nc.sbuf_tensor("%s_%d" % (name, _UNIQ[0]), shape, dtype)


def _consts_setup(pb, cx, consts_d):
    nc = pb.nc
    cx.consts = nc.alloc_sbuf_tensor("consts_sb", [128, NCONST], F32)
    cx.ones = nc.alloc_sbuf_tensor("ones_bf", [128, 128], BF16)
    cx.wbuf = nc.alloc_sbuf_tensor("wbuf", [128, NWSLOT, WSLOT], BF16)
    cx.ps = nc.alloc_psum_tensor("ps_all", [128, 8, 512], F32)
    cx.wrr = 0
    cx.kst_rr = 0
    cx.pref = {}
    cx.vrr = 0
    pb.dma("sp", cx.consts[:], consts_d, [], ["consts"], "consts")
    pb.op("dve", [], ["ones"], lambda e: e.memset(cx.ones[:], 1.0))


def gcol(cx, base, c):
    return cx.consts[:, base + c:base + c + 1]


def load_w(pb, cx, w_ap, kch, ranges):
    slot = cx.wrr % NWSLOT
    cx.wrr += 1
    total = sum(n for _, n in ranges)
    assert kch * total <= WSLOT
    view = cx.wbuf[:, slot, 0:kch * total].rearrange("p (k n) -> p k n", k=kch)
    wv = w_ap.rearrange("(k p) n -> p k n", p=128)
    off = 0
    for i, (c0, n) in enumerate(ranges):
        pb.dma("pool", view[:, :, off:off + n], wv[:, :, c0:c0 + n], [], [("w", slot, i)], "w%d" % slot)
        off += n
    keys = [("w", slot, i) for i in range(len(ranges))]
    return view, keys


def prefetch_w(pb, cx, w_ap, kch, ranges):
    cx.pref[(w_ap.tensor.name, kch, tuple(ranges))] = load_w(pb, cx, w_ap, kch, ranges)


def mm_stage(pb, cx, w_ap, kch, blocks, groups, tiles, xin, consume, banksets):
    rr = 0

    def get_w(bi):
        pk = (w_ap.tensor.name, kch, tuple(blocks[bi]))
        if bi == 0 and pk in cx.pref:
            return cx.pref.pop(pk)
        return load_w(pb, cx, w_ap, kch, blocks[bi])
    ahead = []
    nreq = 0
    for bi, ranges in enumerate(blocks):
        while nreq < len(blocks) and nreq <= bi + NWSLOT - 2:
            ahead.append(get_w(nreq))
            nreq += 1
        view, wkeys = ahead.pop(0)
        for (tid, W) in tiles:
            for gi, grp in enumerate(groups):
                banks = banksets[rr % len(banksets)]
                rr += 1
                assert len(banks) >= len(grp)
                rkeys = list(wkeys) + [xin(k, tid)[1] for k in range(kch)]
                wk = [("ps", banks[i]) for i in range(len(grp))]

                def fn(e, grp=grp, banks=banks, tid=tid, W=W, view=view):
                    ins = None
                    for i, ch in enumerate(grp):
                        for k in range(kch):
                            ins = e.matmul(cx.ps[:, banks[i], 0:W], lhsT=view[:, k, ch * 128:(ch + 1) * 128],
                                           rhs=xin(k, tid)[0], start=(k == 0), stop=(k == kch - 1))
                    return ins
                pb.op("pe", rkeys, wk, fn)
                consume(bi, gi, tid, W, [(cx.ps[:, banks[i], 0:W], ("ps", banks[i])) for i in range(len(grp))])


def rstd_from_sq(pb, cx, sq, sqkeys, nch, W, rstd_ap, rstd_key, bank):
    def fn(e):
        ins = None
        for c in range(nch):
            ins = e.matmul(cx.ps[:, bank, 0:W], lhsT=cx.ones[:], rhs=sq(c), start=(c == 0), stop=(c == nch - 1))
        return ins
    pb.op("pe", list(sqkeys) + ["ones"], [("ps", bank)], fn)
    if RSTD_LNEXP:
        pb.op("act", [("ps", bank), "consts"], [rstd_key],
              lambda e: e.activation(out=rstd_ap, in_=cx.ps[:, bank, 0:W], func=AF.Ln,
                                     bias=cx.consts[:, C_EPS:C_EPS + 1], scale=1.0 / D))
        pb.op("act", [rstd_key], [rstd_key], lambda e: e.activation(out=rstd_ap, in_=rstd_ap, func=AF.Exp, scale=-0.5))
    else:
        pb.op("act", [("ps", bank), "consts"], [rstd_key],
              lambda e: e.activation(out=rstd_ap, in_=cx.ps[:, bank, 0:W], func=AF.Sqrt,
                                     bias=cx.consts[:, C_EPS:C_EPS + 1], scale=1.0 / D))
        pb.op("dve", [rstd_key], [rstd_key], lambda e: e.reciprocal(out=rstd_ap, in_=rstd_ap))


def prenorm_tile(pb, cx, src, W, sqbuf, rstd_ap, rstd_key, bank, outs):
    for c in range(KC):
        a, k = src(c)
        pb.op("act", [k], [("sq", c)], lambda e, a=a, c=c: e.activation(out=sqbuf[:, c, 0:W], in_=a, func=AF.Square))
    rstd_from_sq(pb, cx, lambda c: sqbuf[:, c, 0:W], [("sq", c) for c in range(KC)], KC, W, rstd_ap, rstd_key, bank)
    for (gb, dst, eng) in outs:
        for c in range(KC):
            a, k = src(c)
            da, dk = dst(c)
            pb.op(eng, [k, rstd_key, "consts"], [dk],
                  lambda e, a=a, da=da, c=c, gb=gb: e.scalar_tensor_tensor(
                      out=da, in0=a, scalar=gcol(cx, gb, c), in1=rstd_ap, op0=ALU.mult, op1=ALU.mult))


def evac_y(pb, cx, bank_ap, bank_key, W, y_ap, y_key, sq_ap, sq_key, gain_ap):
    pb.op("act", [bank_key, "consts"], [y_key], lambda e: e.activation(out=y_ap, in_=bank_ap, func=AF.Copy, scale=gain_ap))
    pb.op("act", [bank_key], [sq_key], lambda e: e.activation(out=sq_ap, in_=bank_ap, func=AF.Square))


def ffn_group(pb, cx, h, g0, layer, w_gu, w_dn, gb_pre, gb_post, nxt=None):
    nc = pb.nc
    tiles = [(g0 // TT + i, TT) for i in range(G // TT)]
    with sbt(nc, "a_buf", [128, FC, G], BF16) as a_buf:
        with sbt(nc, "xn", [128, KC, G], BF16) as xn, sbt(nc, "sq", [128, KC, TT], BF16) as sqb, \
                sbt(nc, "rstd", [128, 2, TT], F32) as rstd, sbt(nc, "sg", [128, 2, TT], F32) as sg:
            for i, (tid, W) in enumerate(tiles):
                t0 = tid * TT
                prenorm_tile(pb, cx, lambda c: (h[:, c, t0:t0 + W], ("h", c, tid)), W, sqb, rstd[:, i % 2, :],
                             ("rstd", i % 2), 6 + (i % 2),
                             [(gb_pre, lambda c, tid=tid, t0=t0: (xn[:, c, t0 - g0:t0 - g0 + TT], ("xn", c, tid)), "dve")])
            blocks = [[(jj * 256, 256), (DFF + jj * 256, 256)] for jj in range(FC // 2)]
            sgr = [0]

            def consume(bi, gi, tid, W, banks):
                j = bi * 2 + gi
                s = sgr[0] % 2
                sgr[0] += 1
                (pg, kg), (pu, ku) = banks
                pb.op("act", [kg], [("sg", s)], lambda e: e.activation(out=sg[:, s, :], in_=pg, func=AF.Silu))
                lo = tid * TT - g0
                pb.op("dve", [("sg", s), ku], [("a", j, tid)],
                      lambda e: e.tensor_tensor(out=a_buf[:, j, lo:lo + TT], in0=pu, in1=sg[:, s, :], op=ALU.mult))
            mm_stage(pb, cx, w_gu, KC, blocks, [[0, 2], [1, 3]], tiles,
                     lambda k, tid: (xn[:, k, tid * TT - g0:tid * TT - g0 + TT], ("xn", k, tid)), consume,
                     [[0, 1], [2, 3], [4, 5]])
            prefetch_w(pb, cx, w_dn, FC, [(0, 256)])
        pb.barrier()
        with sbt(nc, "ysb", [128, KC, G], F32) as ysb, sbt(nc, "sq2", [128, 2, KC, TT], BF16) as sq2, \
                sbt(nc, "rstd2", [128, 2, TT], F32) as rstd2, sbt(nc, "tmp", [128, 2, TT], F32) as tmp:
            blocks = [[(jj * 256, 256)] for jj in range(4)]

            def consume2(bi, gi, tid, W, banks):
                oc = bi * 2 + gi
                lo = tid * TT - g0
                ti = (tid - g0 // TT) % 2
                (p, k), = banks
                evac_y(pb, cx, p, k, W, ysb[:, oc, lo:lo + TT], ("y", oc, tid), sq2[:, ti, oc, :], ("sq2", ti, oc), gcol(cx, gb_post, oc))
            mm_stage(pb, cx, w_dn, FC, blocks, [[0], [1]], tiles,
                     lambda k, tid: (a_buf[:, k, tid * TT - g0:tid * TT - g0 + TT], ("a", k, tid)), consume2,
                     [[0], [1], [2], [3]])
            for i, (tid, W) in enumerate(tiles):
                lo = tid * TT - g0
                ti = i % 2
                rstd_from_sq(pb, cx, lambda c: sq2[:, ti, c, :], [("sq2", ti, c) for c in range(KC)], KC, TT,
                             rstd2[:, ti, :], ("rstd2", ti), 6 + ti)
                for c in range(KC):
                    eng, ts_ = ("dve", 0) if c < 6 else ("pool", 1)
                    tk = ("tmp", ts_)
                    ta = tmp[:, ts_, :]
                    ya = ysb[:, c, lo:lo + TT]
                    pb.op(eng, [("y", c, tid), ("rstd2", ti)], [tk],
                          lambda e, ya=ya, ta=ta, ti=ti: e.tensor_tensor(out=ta, in0=ya, in1=rstd2[:, ti, :], op=ALU.mult))
                    ha = h[:, c, tid * TT:tid * TT + TT]
                    hk = ("h", c, tid)
                    pb.op(eng, [tk, hk], [hk], lambda e, ta=ta, ha=ha: e.tensor_tensor(out=ha, in0=ta, in1=ha, op=ALU.add))
            if nxt is not None:
                prefetch_w(pb, cx, *nxt)
        pb.barrier()


def proj_post_group(pb, cx, h, g0, w_ap, xin, gb_post, nxt=None):
    nc = pb.nc
    tiles = [(g0 // TT + i, TT) for i in range(G // TT)]
    with sbt(nc, "ysb", [128, KC, G], F32) as ysb, sbt(nc, "sq2", [128, 2, KC, TT], BF16) as sq2, \
            sbt(nc, "rstd2", [128, 2, TT], F32) as rstd2, sbt(nc, "tmp", [128, 2, TT], F32) as tmp:
        blocks = [[(jj * 256, 256)] for jj in range(4)]

        def consume2(bi, gi, tid, W, banks):
            oc = bi * 2 + gi
            lo = tid * TT - g0
            ti = (tid - g0 // TT) % 2
            (p, k), = banks
            evac_y(pb, cx, p, k, W, ysb[:, oc, lo:lo + TT], ("y", oc, tid), sq2[:, ti, oc, :], ("sq2", ti, oc), gcol(cx, gb_post, oc))
        mm_stage(pb, cx, w_ap, KC, blocks, [[0], [1]], tiles, xin, consume2, [[0], [1], [2], [3]])
        for i, (tid, W) in enumerate(tiles):
            lo = tid * TT - g0
            ti = i % 2
            rstd_from_sq(pb, cx, lambda c: sq2[:, ti, c, :], [("sq2", ti, c) for c in range(KC)], KC, TT,
                         rstd2[:, ti, :], ("rstd2", ti), 6 + ti)
            for c in range(KC):
                eng, ts_ = ("dve", 0) if c < 6 else ("pool", 1)
                tk = ("tmp", ts_)
                ta = tmp[:, ts_, :]
                ya = ysb[:, c, lo:lo + TT]
                pb.op(eng, [("y", c, tid), ("rstd2", ti)], [tk],
                      lambda e, ya=ya, ta=ta, ti=ti: e.tensor_tensor(out=ta, in0=ya, in1=rstd2[:, ti, :], op=ALU.mult))
                ha = h[:, c, tid * TT:tid * TT + TT]
                hk = ("h", c, tid)
                pb.op(eng, [tk, hk], [hk], lambda e, ta=ta, ha=ha: e.tensor_tensor(out=ha, in0=ta, in1=ha, op=ALU.add))
        if nxt is not None:
            prefetch_w(pb, cx, *nxt)
    pb.barrier()


def conv_mixer_group(pb, cx, h, g0, xh, w_in, uh, first):
    raise NotImplementedError


def layer0_group(pb, cx, h, g0, xh, uh, W_, first, nxt=None):
    nc = pb.nc
    tiles = [(g0 // TT + i, TT) for i in range(G // TT)]
    with sbt(nc, "z_buf", [128, KC, G], BF16) as z_buf:
        with sbt(nc, "xn", [128, KC, G], BF16) as xn, sbt(nc, "sq", [128, KC, TT], BF16) as sqb, \
                sbt(nc, "rstd", [128, 2, TT], F32) as rstd, sbt(nc, "hs", [128, 2, TT], F32) as hs, \
                sbt(nc, "u", [128, 2, TT + 2], F32) as ub, sbt(nc, "cv", [128, 2, TT], F32) as cv, \
                sbt(nc, "xnh", [128, KC, 2], BF16) as xnh:
            if first:
                prenorm_tile(pb, cx, lambda c: (xh[:, c, :], ("xh", c)), 2, sqb, rstd[:, 0, 0:2], ("rstd", 0), 6,
                             [(C_MIXPRE0, lambda c: (xnh[:, c, :], ("xnh", c)), "dve")])
            for i, (tid, W) in enumerate(tiles):
                t0 = tid * TT
                prenorm_tile(pb, cx, lambda c: (h[:, c, t0:t0 + W], ("h", c, tid)), W, sqb, rstd[:, i % 2, :],
                             ("rstd", i % 2), 6 + (i % 2),
                             [(C_MIXPRE0, lambda c, tid=tid, t0=t0: (xn[:, c, t0 - g0:t0 - g0 + TT], ("xn", c, tid)), "dve")])
            blocks = [[(jj * 256, 256), (D + jj * 256, 256), (2 * D + jj * 256, 256)] for jj in range(4)]
            rr = [0]

            def consume(bi, gi, tid, W, banks):
                j = bi * 2 + gi
                s = rr[0] % 2
                rr[0] += 1
                (pbk, kb), (pck, kc_), (phk, kh) = banks
                if tid < 0:
                    pb.op("act", [kh], [("hs", s)], lambda e: e.activation(out=hs[:, s, 0:2], in_=phk, func=AF.Copy))
                    pb.op("dve", [("hs", s), kc_], [("uh", j)],
                          lambda e: e.tensor_tensor(out=uh[:, j, :], in0=pck, in1=hs[:, s, 0:2], op=ALU.mult))
                    return
                pb.op("act", [kh], [("hs", s)], lambda e: e.activation(out=hs[:, s, :], in_=phk, func=AF.Copy))
                pb.op("dve", [("uh", j)], [("u", s)], lambda e: e.tensor_copy(out=ub[:, s, 0:2], in_=uh[:, j, :]))
                pb.op("dve", [("hs", s), kc_, ("u", s)], [("u", s)],
                      lambda e: e.tensor_tensor(out=ub[:, s, 2:TT + 2], in0=pck, in1=hs[:, s, :], op=ALU.mult))
                pb.op("dve", [("u", s)], [("uh", j)], lambda e: e.tensor_copy(out=uh[:, j, :], in_=ub[:, s, TT:TT + 2]))
                cw = lambda t: cx.consts[:, C_CONV + t * 8 + j:C_CONV + t * 8 + j + 1]
                pb.op("dve", [("u", s), "consts"], [("cv", s)],
                      lambda e: e.tensor_scalar(out=cv[:, s, :], in0=ub[:, s, 2:TT + 2], scalar1=cw(2), scalar2=None, op0=ALU.mult))
                pb.op("dve", [("u", s), ("cv", s), "consts"], [("cv", s)],
                      lambda e: e.scalar_tensor_tensor(out=cv[:, s, :], in0=ub[:, s, 1:TT + 1], scalar=cw(1), in1=cv[:, s, :],
                                                       op0=ALU.mult, op1=ALU.add))
                pb.op("dve", [("u", s), ("cv", s), "consts"], [("cv", s)],
                      lambda e: e.scalar_tensor_tensor(out=cv[:, s, :], in0=ub[:, s, 0:TT], scalar=cw(0), in1=cv[:, s, :],
                                                       op0=ALU.mult, op1=ALU.add))
                lo = tid * TT - g0
                pb.op("dve", [("cv", s), kb], [("z", j, tid)],
                      lambda e: e.tensor_tensor(out=z_buf[:, j, lo:lo + TT], in0=pbk, in1=cv[:, s, :], op=ALU.mult))

            def xin(k, tid):
                if tid < 0:
                    return (xnh[:, k, :], ("xnh", k))
                return (xn[:, k, tid * TT - g0:tid * TT - g0 + TT], ("xn", k, tid))
            tl = ([(-1, 2)] if first else []) + tiles
            mm_stage(pb, cx, W_["conv_w_in"], KC, blocks, [[0, 2, 4], [1, 3, 5]], tl, xin, consume,
                     [[0, 1, 2], [3, 4, 5]])
            prefetch_w(pb, cx, W_["conv_w_out"], KC, [(0, 256)])
        pb.barrier()
        proj_post_group(pb, cx, h, g0, W_["conv_w_out"],
                        lambda k, tid: (z_buf[:, k, tid * TT - g0:tid * TT - g0 + TT], ("z", k, tid)), C_MIXPOST0,
                        nxt=(W_["gu0"], KC, [(0, 256), (DFF, 256)]))
    ffn_group(pb, cx, h, g0, 0, W_["gu0"], W_["dn0"], C_FFNPRE0, C_FFNPOST0, nxt=nxt)


def rope_tables(pb, cx, pos_d, cosT, sinT):
    nc = pb.nc
    C1 = 6.28125
    C2 = float(2.0 * np.pi - 6.28125)
    CW = TOWN // 4
    with sbt(nc, "posi", [128, CW], I32) as posi, sbt(nc, "ang", [128, CW], F32) as ang, \
            sbt(nc, "red", [128, CW], F32) as red, sbt(nc, "kf", [128, CW], F32) as kf, \
            sbt(nc, "ki", [128, CW], I32) as ki, sbt(nc, "tres", [128, 2, CW], F32) as tres:
        pb.dma("sp", posi[:], pos_d, [], ["posi"], "posi")
        pb.op("dve", ["posi"], ["ang"], lambda e: e.tensor_copy(out=ang[:], in_=posi[:]))
        pb.op("dve", ["ang", "consts"], ["ang"],
              lambda e: e.tensor_scalar(out=ang[:], in0=ang[:], scalar1=cx.consts[:, C_INVF:C_INVF + 1], scalar2=None, op0=ALU.mult))

        def sin_of(ti, dst, dkey, shift):
            pb.op("dve", ["ang"], ["kf"],
                  lambda e: e.tensor_scalar(out=kf[:], in0=ang[:], scalar1=float(shift), scalar2=float(1.0 / (2.0 * np.pi)),
                                            op0=ALU.add, op1=ALU.mult))
            pb.op("dve", ["kf"], ["ki"], lambda e: e.tensor_copy(out=ki[:], in_=kf[:]))
            pb.op("dve", ["ki"], ["kf"], lambda e: e.tensor_copy(out=kf[:], in_=ki[:]))
            pb.op("dve", ["ang"], ["red"],
                  lambda e: e.tensor_scalar(out=red[:], in0=ang[:], scalar1=float(shift), scalar2=None, op0=ALU.add))
            pb.op("dve", ["kf", "red"], ["red"],
                  lambda e: e.scalar_tensor_tensor(out=red[:], in0=kf[:], scalar=-C1, in1=red[:], op0=ALU.mult, op1=ALU.add))
            pb.op("dve", ["kf", "red"], ["red"],
                  lambda e: e.scalar_tensor_tensor(out=red[:], in0=kf[:], scalar=-C2, in1=red[:], op0=ALU.mult, op1=ALU.add))
            pb.op("dve", ["red"], ["kf"],
                  lambda e: e.tensor_scalar(out=kf[:], in0=red[:], scalar1=float(np.pi), scalar2=float(2.0 * np.pi),
                                            op0=ALU.is_gt, op1=ALU.mult))
            pb.op("dve", ["red", "kf"], ["red"], lambda e: e.tensor_tensor(out=red[:], in0=red[:], in1=kf[:], op=ALU.subtract))
            pb.op("dve", ["red"], ["kf"],
                  lambda e: e.tensor_scalar(out=kf[:], in0=red[:], scalar1=float(-np.pi), scalar2=float(2.0 * np.pi),
                                            op0=ALU.is_lt, op1=ALU.mult))
            pb.op("dve", ["red", "kf"], ["red"], lambda e: e.tensor_tensor(out=red[:], in0=red[:], in1=kf[:], op=ALU.add))
            pb.op("dve", ["red"], ["red"],
                  lambda e: e.tensor_scalar(out=red[:], in0=red[:], scalar1=3.1415925, scalar2=-3.1415925, op0=ALU.min, op1=ALU.max))
            pb.op("act", ["red"], [("tres", ti)], lambda e: e.activation(out=tres[:, ti, :], in_=red[:], func=AF.Sin))

            def spread(e):
                ins = None
                for q in range(4):
                    for a in range(4):
                        ins = e.tensor_copy(out=dst[32 * a:32 * a + 32, q * CW:(q + 1) * CW], in_=tres[32 * q:32 * q + 32, ti, :])
                return ins
            pb.op("dve", [("tres", ti)], [dkey], spread)
        sin_of(0, sinT, "sinT", 0.0)
        sin_of(1, cosT, "cosT", float(np.pi / 2))
        pb.barrier()


def res_view(buf_ap_2d, d, t0, W):
    v = buf_ap_2d.rearrange("p (r j) -> p j r", r=d)
    return v[:, t0 // d:(t0 + W) // d, :]


def kvq_phase(pb, cx, h, W_, pos_d, KT_d, V_d, QT_d, halo=False, after_prenorm=None, nxt=None):
    nc = pb.nc
    ntile = TOWN // TT
    with sbt(nc, "cosT", [128, TOWN], F32) as cosT, sbt(nc, "sinT", [128, TOWN], F32) as sinT:
        rope_tables(pb, cx, pos_d, cosT, sinT)
        with sbt(nc, "kn", [128, KC, TOWN], BF16) as kn, sbt(nc, "sq", [128, KC, TT], BF16) as sqb, \
                sbt(nc, "rstd", [128, 4, TT], F32) as rstd, \
                sbt(nc, "kst", [128, 1, 2, 2, TOWN], BF16) as kst, \
                sbt(nc, "vst", [128, 2, 4, 512], BF16) as vst, \
                sbt(nc, "rtmp", [128, 2, 4, TT], F32) as rtmp:
            for which in (("kv",) if halo else ("kv", "q")):
                gb = C_KVN if which == "kv" else C_MIXPRE1
                for i in range(ntile):
                    t0 = i * TT
                    if which == "kv":
                        prenorm_tile(pb, cx, lambda c: (h[:, c, t0:t0 + TT], ("h", c, i)), TT, sqb, rstd[:, i, :],
                                     ("rstd", i), 6 + (i % 2),
                                     [(gb, lambda c, i=i, t0=t0: (kn[:, c, t0:t0 + TT], ("kn", c, i)), "dve")])
                    else:
                        for c in range(KC):
                            pb.op("dve", [("h", c, i), ("rstd", i), "consts"], [("kn", c, i)],
                                  lambda e, c=c, i=i, t0=t0: e.scalar_tensor_tensor(
                                      out=kn[:, c, t0:t0 + TT], in0=h[:, c, t0:t0 + TT], scalar=gcol(cx, gb, c), in1=rstd[:, i, :],
                                      op0=ALU.mult, op1=ALU.mult))
                if which == "kv" and after_prenorm is not None:
                    after_prenorm()
                w_ap = W_["w_kv"] if which == "kv" else W_["w_q"]
                dst_d = KT_d if which == "kv" else QT_d
                rs = [0]

                def consume(bi, gi, tid, W, banks, which=which, dst_d=dst_d, meta=None, part=False):
                    g, mm, par = meta[bi]
                    m = 2 * mm + gi
                    d = BRANCH_D[g]
                    s = rs[0] % 2
                    rs[0] += 1
                    (pA, kA), (pB, kB) = banks
                    t0 = tid * TT
                    cs_ = cosT[:, t0:t0 + TT]
                    sn_ = sinT[:, t0:t0 + TT]
                    for idx, (src, sk, tab, tk) in enumerate(((pA, kA, cs_, "cosT"), (pB, kB, sn_, "sinT"),
                                                               (pB, kB, cs_, "cosT"), (pA, kA, sn_, "sinT"))):
                        pb.op("dve", [sk, tk], [("rtmp", s, idx)],
                              lambda e, src=src, tab=tab, idx=idx: e.tensor_tensor(out=rtmp[:, s, idx, :], in0=src, in1=tab, op=ALU.mult))
                    for ab, (i_a, i_b, op_, eng) in enumerate(((0, 1, ALU.subtract, "dve"), (2, 3, ALU.add, "pool"))):
                        if part:
                            dv = kst[:, par, gi, ab, 0:TT].rearrange("p (r j) -> p r j", r=d)
                        else:
                            dv = kst[:, par, gi, ab, :].rearrange("p (r j) -> p r j", r=d)[:, :, t0 // d:(t0 + TT) // d]
                        kkeys = [("kst", par, gi, ab, t) for t in range(ntile)] if part else [("kst", par, gi, ab, tid)]
                        pb.op(eng, [("rtmp", s, i_a), ("rtmp", s, i_b)], kkeys,
                              lambda e, dv=dv, i_a=i_a, i_b=i_b, op_=op_: e.tensor_tensor(
                                  out=dv, in0=rtmp[:, s, i_a, :].rearrange("p (j r) -> p r j", r=d),
                                  in1=rtmp[:, s, i_b, :].rearrange("p (j r) -> p r j", r=d), op=op_))
                    if tid == ntile - 1:
                        for ab in range(2):
                            cc = 4 * ab + m
                            if part:
                                lo = TT - 128 * d
                                pb.dma("sp", dst_d[g, cc, :, 0:128 * d], kst[:, par, gi, ab, lo:TT],
                                       [("kst", par, gi, ab, t) for t in range(ntile)], [(which, g, cc)], "kst%d%d%d" % (par, gi, ab))
                            else:
                                pb.dma("sp", dst_d[g, cc], kst[:, par, gi, ab, :], [("kst", par, gi, ab, t) for t in range(ntile)],
                                       [(which, g, cc)], "kst%d%d%d" % (par, gi, ab))
                xin = lambda k, tid: (kn[:, k, tid * TT:tid * TT + TT], ("kn", k, tid))

                def kq_blocks(gl):
                    bl, meta = [], []
                    for g in gl:
                        for mm in range(2):
                            bl.append([(g * D + mm * 256, 256), (g * D + 512 + mm * 256, 256)])
                            meta.append((g, mm, 0))
                            cx.kst_rr += 1
                    return bl, meta
                if halo:
                    bl, meta = kq_blocks([0, 1])
                    mm_stage(pb, cx, w_ap, KC, bl, [[0, 2], [1, 3]], [(ntile - 1, TT)], xin,
                             lambda bi, gi, tid, W, banks, meta=meta: consume(bi, gi, tid, W, banks, meta=meta, part=True),
                             [[0, 1], [2, 3], [4, 5]])
                    bl, meta = kq_blocks([2])
                    mm_stage(pb, cx, w_ap, KC, bl, [[0, 2], [1, 3]], [(i, TT) for i in range(ntile)], xin,
                             lambda bi, gi, tid, W, banks, meta=meta: consume(bi, gi, tid, W, banks, meta=meta, part=False),
                             [[0, 1], [2, 3], [4, 5]])
                else:
                    bl, meta = kq_blocks([0, 1, 2])
                    mm_stage(pb, cx, w_ap, KC, bl, [[0, 2], [1, 3]], [(i, TT) for i in range(ntile)], xin,
                             lambda bi, gi, tid, W, banks, meta=meta: consume(bi, gi, tid, W, banks, meta=meta, part=False),
                             [[0, 1], [2, 3], [4, 5]])
                if which == "kv":
                    vi = [0]
                    vjobs = [(g, half) for g in range(3) for half in range(2)]
                    vahead = []
                    vreq = [0]

                    def v_request(upto):
                        while vreq[0] < len(vjobs) and vreq[0] <= upto:
                            g_, half_ = vjobs[vreq[0]]
                            vahead.append(load_w(pb, cx, w_ap, KC, [(3 * D + g_ * D + half_ * 512, 512)]))
                            vreq[0] += 1
                    for g in range(3):
                        d = BRANCH_D[g]
                        nb = TOWN // (128 * d)
                        if halo:
                            vblocks = [(r, r, nb - 1) for r in range(d)]
                        else:
                            vblocks = [(blk, blk // nb, blk % nb) for blk in range(16)]
                        for half in range(2):
                            v_request(2 * g + half + NWSLOT - 2)
                            view, wkeys = vahead.pop(0)
                            pend = []
                            for (di, r, n) in vblocks:
                                bank = vi[0] % 4
                                vi[0] += 1
                                base = n * 128 * d
                                if not pend:
                                    vslot = cx.vrr % 2
                                    cx.vrr += 1

                                def fn(e, base=base, d=d, bank=bank, view=view, r=r):
                                    ins = None
                                    for kk in range(KC):
                                        lhsT = kn[:, kk, base:base + 128 * d].rearrange("p (i r) -> p i r", r=d)[:, :, r]
                                        ins = e.matmul(cx.ps[:, bank, :], lhsT=lhsT, rhs=view[:, kk, :],
                                                       start=(kk == 0), stop=(kk == KC - 1))
                                    return ins
                                pb.op("pe", list(wkeys) + [("kn", kk, t) for kk in range(KC) for t in range(ntile)],
                                      [("ps", bank)], fn)
                                vs_i = len(pend)
                                pb.op("act", [("ps", bank)], [("vst", vslot, vs_i)],
                                      lambda e, bank=bank, vslot=vslot, vs_i=vs_i: e.activation(
                                          out=vst[:, vslot, vs_i, :], in_=cx.ps[:, bank, :], func=AF.Copy))
                                pend.append(di)
                                if len(pend) == 4 or di == vblocks[-1][0]:
                                    b0 = pend[0]
                                    nn = len(pend)
                                    pb.dma("sp", V_d[g, :, b0:b0 + nn, half * 512:half * 512 + 512], vst[:, vslot, 0:nn, :],
                                           [("vst", vslot, x) for x in range(nn)], [("V", g, half, b0 // 4)], "vst%d" % vslot)
                                    pend = []
                if nxt is not None and which == ("kv" if halo else "q"):
                    prefetch_w(pb, cx, *nxt)
                pb.barrier()


def attention_phase(pb, cx, o_all, QT_d, KT_d, KTh_d, V_d, Vh_d, masks_d, nxt=None):
    nc = pb.nc
    with sbt(nc, "masks", [128, 4, 512], BF16) as masks, sbt(nc, "ident", [128, 128], BF16) as ident, \
            sbt(nc, "accn", [128, TOWN], F32) as accn, sbt(nc, "accd", [128, TOWN], F32) as accd, \
            sbt(nc, "qb", [128, 2, TOWN], BF16) as qb, sbt(nc, "kb", [128, 2, 2 * TOWN], BF16) as kb, \
            sbt(nc, "vb", [128, 2, 32, 192], BF16) as vb, sbt(nc, "pT", [128, 4, 512], BF16) as pT:
        for uu in range(2):
            pb.op("dve", [], [("vones", uu)], lambda e, uu=uu: e.memset(vb[:, uu, :, 64:128], 1.0))
        pb.dma("sp", masks[:], masks_d[0], [], ["masks"], "masks")
        pb.dma("sp", ident[:], masks_d[1][:, 0, 0:128], [], ["ident"], "masks")
        M_HPPP, M_PPPP, M_HHHH, M_CCCC = 0, 1, 2, 3
        ui = 0
        sc_rr = 0
        nd_rr = 0
        pt_rr = 0
        pending = []
        LAG = 2

        def flush():
            while pending:
                pending.pop(0)[1]()

        def release():
            while pending and (pending[0][0] == "evac" or sum(1 for kd, _ in pending if kd == "pv") > LAG):
                pending.pop(0)[1]()
        for c in range(KC):
            for g in DBG_BRANCHES:
                d = BRANCH_D[g]
                nb = TOWN // (128 * d)
                u = ui % 2
                ui += 1
                HOFF = TOWN
                mq, eo = c // 2, 64 * (c % 2)
                qkeys = []
                for hh in range(2):
                    for ab in range(2):
                        prt = slice(64 * hh + 32 * ab, 64 * hh + 32 * ab + 32)
                        rws = slice(eo + 32 * hh, eo + 32 * hh + 32)
                        cc = 4 * ab + mq
                        pb.dma("sp", qb[prt, u, :], QT_d[g, cc, rws, :], [("q", g, cc)], [("qb", u, hh, ab)], "qb%d" % u)
                        pb.dma("sp", kb[prt, u, HOFF:HOFF + TOWN], KT_d[g, cc, rws, :], [("kv", g, cc)], [("kb", u, hh, ab)], "kb%d" % u)
                        pb.dma("sp", kb[prt, u, 0:128 * d], KTh_d[g, cc, rws, 0:128 * d], [], [("kbh", u, hh, ab)], "kbh%d" % u)
                        qkeys += [("qb", u, hh, ab), ("kb", u, hh, ab), ("kbh", u, hh, ab)]
                for hh in range(2):
                    vc = slice(128 * hh, 128 * hh + 64)
                    sc_ = slice(c * 128 + 64 * hh, c * 128 + 64 * hh + 64)
                    pb.dma("sp", vb[:, u, 16:32, vc], V_d[g, :, :, sc_],
                           [("V", g, hf, bb) for hf in range(2) for bb in range(4)], [("vb", u, hh)], "vb%d" % u)
                    pb.dma("sp", vb[:, u, 0:d, vc], Vh_d[g, :, 0:d, sc_], [], [("vbh", u, hh)], "vbh%d" % u)
                for sb in range(4):
                    nbank, dbank = (4, 5) if nd_rr % 2 == 0 else (6, 7)
                    nd_rr += 1
                    for e_ in range(2):
                        pl = slice(64 * e_, 64 * e_ + 64)
                        for side in range(2):
                            sbank = sc_rr % 4
                            sc_rr += 1
                            ps_s = cx.ps[:, sbank, :]
                            blks = [4 * sb + x for x in range(4)]
                            if side == 1:
                                mk = M_CCCC
                            else:
                                fl = [(bq % nb) == 0 for bq in blks]
                                mk = M_HHHH if all(fl) else (M_HPPP if fl[0] else M_PPPP)

                            def kview(bq):
                                r, n = bq // nb, bq % nb
                                if side == 1:
                                    o = HOFF + bq * 128
                                elif n == 0:
                                    o = r * 128
                                else:
                                    o = HOFF + (bq - 1) * 128
                                return o

                            def vidx(bq):
                                r, n = bq // nb, bq % nb
                                if side == 1:
                                    return 16 + bq
                                elif n == 0:
                                    return r
                                return 16 + bq - 1

                            def fscore(e, ps_s=ps_s, mk=mk, kview=kview, blks=blks, pl=pl, e_=e_, u=u):
                                e.matmul(ps_s, lhsT=ident[:], rhs=masks[:, mk, :], start=True, stop=False)
                                ins = None
                                for x, bq in enumerate(blks):
                                    o = kview(bq)
                                    ins = e.matmul(ps_s[:, x * 128:(x + 1) * 128], lhsT=kb[pl, u, o:o + 128],
                                                   rhs=qb[pl, u, bq * 128:(bq + 1) * 128], start=False, stop=(x == 3),
                                                   tile_position=(64 * e_, 0))
                                return ins
                            pb.op("pe", qkeys + ["masks", "ident"], [("ps", sbank)], fscore)
                            pslot = pt_rr % 4
                            pt_rr += 1
                            pb.op("act", [("ps", sbank)], [("pT", pslot)],
                                  lambda e, ps_s=ps_s, pslot=pslot: e.activation(out=pT[:, pslot, :], in_=ps_s, func=AF.Exp,
                                                                                 scale=0.125))

                            hbank = nbank if e_ == 0 else dbank
                            vlist = [vidx(bq) for bq in blks]

                            def do_pv(vlist=vlist, pslot=pslot, e_=e_, side=side, u=u, hbank=hbank):
                                def fpv(e):
                                    ins = None
                                    for x in range(4):
                                        cs = slice(x * 128, (x + 1) * 128)
                                        ins = e.matmul(cx.ps[:, hbank, cs], lhsT=vb[:, u, vlist[x], 64 * e_:64 * e_ + 128],
                                                       rhs=pT[:, pslot, cs], start=(side == 0 and x == 0), stop=(side == 1 and x == 3))
                                    return ins
                                pb.op("pe", [("pT", pslot), ("vb", u, 0), ("vb", u, 1), ("vbh", u, 0), ("vbh", u, 1), ("vones", u)],
                                      [("ps", hbank)], fpv)
                            pending.append(("pv", do_pv))
                            release()
                    def do_evac(d=d, sb=sb, g=g, nbank=nbank, dbank=dbank):
                        def tview(acc, pr):
                            if d == 1:
                                return acc[pr, sb * 512:(sb + 1) * 512]
                            if d == 4:
                                return acc[pr, :].rearrange("p (j r) -> p r j", r=4)[:, sb, :]
                            return acc[pr, :].rearrange("p (j r) -> p r j", r=16)[:, 4 * sb:4 * sb + 4, :]

                        def sview(bank, pr):
                            if d == 16:
                                return cx.ps[pr, bank, :].rearrange("p (x j) -> p x j", x=4)
                            return cx.ps[pr, bank, :]
                        lo_, hi_ = slice(0, 64), slice(64, 128)
                        moves = ((accn, "accn", lo_, nbank, lo_), (accn, "accn", hi_, dbank, hi_),
                                 (accd, "accd", lo_, nbank, hi_), (accd, "accd", hi_, dbank, lo_))
                        for (acc, an, pr_o, bank, pr_i) in moves:
                            vo = tview(acc, pr_o)
                            si = sview(bank, pr_i)
                            half = 0 if pr_o is lo_ else 1
                            if g == DBG_BRANCHES[0]:
                                pb.op("dve", [("ps", bank)], [(an, sb, half)], lambda e, vo=vo, si=si: e.tensor_copy(out=vo, in_=si))
                            else:
                                kk_ = [(an, x, half) for x in range(4)]
                                pb.op("dve", [("ps", bank)] + kk_, kk_,
                                      lambda e, vo=vo, si=si: e.tensor_tensor(out=vo, in0=si, in1=vo, op=ALU.add))
                    pending.append(("evac", do_evac))
            flush()
            for t in range(TOWN // TT):
                cs = slice(t * TT, (t + 1) * TT)
                dk = [("accd", t, 0), ("accd", t, 1)]
                ak = [("accn", t, 0), ("accn", t, 1)]
                pb.op("act", dk, dk, lambda e, cs=cs: e.activation(out=accd[:, cs], in_=accd[:, cs], func=AF.Ln))
                pb.op("act", dk, dk, lambda e, cs=cs: e.activation(out=accd[:, cs], in_=accd[:, cs], func=AF.Exp, scale=-1.0))
                pb.op("pool", ak + dk, [("o", c, t)],
                      lambda e, c=c, cs=cs: e.tensor_tensor(out=o_all[:, c, cs], in0=accn[:, cs], in1=accd[:, cs], op=ALU.mult))
        if nxt is not None:
            prefetch_w(pb, cx, *nxt)
        pb.barrier()


def declare_weights(nc, names):
    shapes = dict(conv_w_in=[D, 3 * D], conv_w_out=[D, D], gu0=[D, 2 * DFF], dn0=[DFF, D], w_kv=[D, 6 * D], w_q=[D, 3 * D],
                  w_o=[D, D], gu1=[D, 2 * DFF], dn1=[DFF, D])
    return {n: nc.dram_tensor(n, shapes[n], F32, kind="ExternalInput").ap() for n in names}


def build_fused():
    nc = bass.Bass("TRN2", target_bir_lowering=False)
    xT = nc.dram_tensor("xT", [128, KC, 2 * TOWN], F32, kind="ExternalInput").ap()
    pos = nc.dram_tensor("pos", [128, 2, TOWN // 4], I32, kind="ExternalInput").ap()
    consts_d = nc.dram_tensor("consts", [128, NCONST], F32, kind="ExternalInput").ap()
    masks_d = nc.dram_tensor("masks", [2, 128, 4, 512], BF16, kind="ExternalInput").ap()
    W_ = declare_weights(nc, ["conv_w_in", "conv_w_out", "gu0", "dn0", "w_kv", "w_q", "w_o", "gu1", "dn1"])
    outT = nc.dram_tensor("outT", [128, KC, TOWN], F32, kind="ExternalOutput").ap()
    KT = nc.dram_tensor("KT_s", [3, KC, 128, TOWN], BF16).ap()
    QT = nc.dram_tensor("QT_s", [3, KC, 128, TOWN], BF16).ap()
    Vs = nc.dram_tensor("Vs_s", [3, 128, 16, D], BF16).ap()
    KTh = nc.dram_tensor("KTh_s", [3, KC, 128, TOWN], BF16).ap()
    Vh = nc.dram_tensor("Vh_s", [3, 128, 16, D], BF16).ap()
    pb = PB(nc)
    cx = Ctx()
    _consts_setup(pb, cx, consts_d)
    h = nc.alloc_sbuf_tensor("h_res", [128, KC, TOWN], F32)
    uh = nc.alloc_sbuf_tensor("uh", [128, KC, 2], F32)
    pb.op("dve", [], [("uh", j) for j in range(KC)], lambda e: e.memset(uh[:], 0.0))
    def load_x(part):
        for i in range(TOWN // TT):
            pb.dma("sp", h[:, :, i * TT:(i + 1) * TT], xT[:, :, part * TOWN + i * TT:part * TOWN + (i + 1) * TT], [],
                   [("h", c, i) for c in range(KC)], "hin%d" % i)
    load_x(0)
    for part in range(2):
        w_in0 = (W_["conv_w_in"], KC, [(0, 256), (D, 256), (2 * D, 256)])
        w_k0 = (W_["w_kv"], KC, [(0, 256), (512, 256)])
        for gi in range(TOWN // G):
            layer0_group(pb, cx, h, gi * G, None, uh, W_, False, nxt=(w_in0 if gi == 0 else w_k0))
        if part == 0:
            kvq_phase(pb, cx, h, W_, pos[:, 0, :], KTh, Vh, None, halo=True, after_prenorm=lambda: load_x(1), nxt=w_in0)
        else:
            kvq_phase(pb, cx, h, W_, pos[:, 1, :], KT, Vs, QT, halo=False)
        if STOP_AFTER == part + 1:
            for i in range(TOWN // TT):
                pb.dma("sp", outT[:, :, i * TT:(i + 1) * TT], h[:, :, i * TT:(i + 1) * TT], [("h", c, i) for c in range(KC)], [], "hout%d" % i)
            pb.barrier()
            return nc
    with sbt(nc, "o_all", [128, KC, TOWN], BF16) as o_all:
        attention_phase(pb, cx, o_all, QT, KT, KTh, Vs, Vh, masks_d, nxt=(W_["w_o"], KC, [(0, 256)]))
        if STOP_AFTER == 3:
            for i in range(TOWN // TT):
                pb.dma("sp", outT[:, :, i * TT:(i + 1) * TT], h[:, :, i * TT:(i + 1) * TT], [("h", c, i) for c in range(KC)], [], "hout%d" % i)
            pb.barrier()
            return nc
        for gi in range(TOWN // G):
            proj_post_group(pb, cx, h, gi * G, W_["w_o"],
                            lambda k, tid: (o_all[:, k, tid * TT:tid * TT + TT], ("o", k, tid)), C_MIXPOST1,
                            nxt=((W_["w_o"], KC, [(0, 256)]) if gi == 0 else (W_["gu1"], KC, [(0, 256), (DFF, 256)])))
    for gi in range(TOWN // G):
        ffn_group(pb, cx, h, gi * G, 1, W_["gu1"], W_["dn1"], C_FFNPRE1, C_FFNPOST1,
                  nxt=((W_["gu1"], KC, [(0, 256), (DFF, 256)]) if gi == 0 else None))
        for i in range(gi * G // TT, (gi + 1) * G // TT):
            pb.dma("sp", outT[:, :, i * TT:(i + 1) * TT], h[:, :, i * TT:(i + 1) * TT], [("h", c, i) for c in range(KC)], [], "hout%d" % i)
    pb.barrier()
    return nc


def _fm(v):
    return np.ascontiguousarray(np.asarray(v, np.float32).reshape(KC, 128).T)


def make_consts(inp):
    c = np.zeros((128, NCONST), np.float32)
    for l in range(2):
        c[:, 32 * l + 0:32 * l + 8] = _fm(inp["mix_norm_pre"][l])
        c[:, 32 * l + 8:32 * l + 16] = _fm(inp["mix_norm_post"][l])
        c[:, 32 * l + 16:32 * l + 24] = _fm(inp["ffn_norm_pre"][l])
        c[:, 32 * l + 24:32 * l + 32] = _fm(inp["ffn_norm_post"][l])
    c[:, C_KVN:C_KVN + 8] = _fm(inp["kv_norm"])
    for t in range(3):
        c[:, C_CONV + 8 * t:C_CONV + 8 * t + 8] = _fm(inp["conv_w"][0, t])
    half = 32
    inv_freq = (10000.0 ** (-np.arange(half, dtype=np.float32) / half)).astype(np.float32)
    p = np.arange(128)
    c[:, C_INVF] = inv_freq[p % 32]
    c[:, C_NSGN] = np.where((p // 32) % 2 == 0, 1.0, -1.0)
    c[:, C_EPS] = EPS
    c[:, C_NEG1] = -1.0
    return c


def rope_perm():
    perm = np.zeros(3 * D, np.int64)
    for g in range(3):
        for cc in range(8):
            ab, m = cc // 4, cc % 4
            for p in range(128):
                perm[g * D + cc * 128 + p] = g * D + (4 * m + p // 32) * 64 + 32 * ab + (p % 32)
    return perm


def make_masks(rank):
    i = np.arange(128)
    kj, qi = i[:, None], i[None, :]
    m_prev = np.where(kj >= qi, 0.0, NEG).astype(np.float32)
    m_cur = np.where(kj <= qi, 0.0, NEG).astype(np.float32)
    m_halo = m_prev if rank == 1 else np.full((128, 128), NEG, np.float32)
    def cat(*ms):
        return np.concatenate(ms, axis=1)
    m = np.stack([cat(m_halo, m_prev, m_prev, m_prev), cat(m_prev, m_prev, m_prev, m_prev),
                  cat(m_halo, m_halo, m_halo, m_halo), cat(m_cur, m_cur, m_cur, m_cur)], axis=1)
    idn = np.zeros((128, 4, 512), np.float32)
    idn[:, 0, 0:128] = np.eye(128, dtype=np.float32)
    return np.stack([m, idn], axis=0).astype(ml_dtypes.bfloat16)


_CACHE = {}


def _get(name, fn):
    if name not in _CACHE:
        _CACHE[name] = fn()
    return _CACHE[name]


def in_maps_fused(I, cores):
    inp = {k: np.asarray(I[k]) for k in ("mix_norm_pre", "mix_norm_post", "ffn_norm_pre", "ffn_norm_post", "kv_norm", "conv_w")}
    x = np.asarray(I["x"], np.float32)
    positions = np.asarray(I["positions"], np.int32)
    consts = make_consts(inp)
    f32c = lambda a: np.ascontiguousarray(np.asarray(a, np.float32))
    perm = rope_perm()
    wkv = np.asarray(I["w_kv"], np.float32)
    wkv_p = np.concatenate([wkv[:, :3 * D][:, perm], wkv[:, 3 * D:]], axis=1)
    W = dict(conv_w_in=f32c(I["conv_w_in"][0]), conv_w_out=f32c(I["conv_w_out"][0]), gu0=f32c(I["ffn_w_gate_up"][0]),
             dn0=f32c(I["ffn_w_down"][0]), w_kv=f32c(wkv_p), w_q=f32c(np.asarray(I["w_q"][0], np.float32)[:, perm]),
             w_o=f32c(I["w_o"][0]), gu1=f32c(I["ffn_w_gate_up"][1]), dn1=f32c(I["ffn_w_down"][1]))
    maps = []
    for core in cores:
        b, rank = core // 2, core % 2
        xs = np.zeros((2 * TOWN, D), np.float32)
        ps = np.zeros((2 * TOWN,), np.int32)
        if rank == 1:
            xs[:] = x[b]
            ps[:] = positions[b]
        else:
            xs[TOWN:] = x[b, 0:TOWN]
            ps[TOWN:] = positions[b, 0:TOWN]
        xT = np.ascontiguousarray(xs.T.reshape(KC, 128, 2 * TOWN).transpose(1, 0, 2))
        pos = np.ascontiguousarray(np.repeat(ps.reshape(2, 4, 1, TOWN // 4), 32, axis=2).reshape(2, 128, TOWN // 4).transpose(1, 0, 2))
        m = dict(xT=xT, pos=pos, consts=consts, masks=make_masks(rank))
        m.update(W)
        maps.append(m)
    return maps


def kernel(**I):
    ncores = 8
    cores = list(range(ncores))
    res = run_bass_kernel_spmd(build_fused(), in_maps_fused(I, cores), core_ids=cores)
    out = np.zeros((4, 2 * TOWN, D), np.float32)
    for core in cores:
        b, rank = core // 2, core % 2
        oT = np.asarray(res.results[core]["outT"])
        out[b, rank * TOWN:(rank + 1) * TOWN] = oT.transpose(2, 1, 0).reshape(TOWN, D)
    return out
```
